# Optimizing a Trainium2 kernel written in Bass

```python
import jax, jax.numpy as jnp
from jax import lax
import numpy as np

D_MODEL = 1024
BATCH = 4
SEQ = 8192
DEPTH = 1

D_SSM = D_MODEL
SSM_HEAD_DIM = 64
SSM_HEADS = D_SSM // SSM_HEAD_DIM
SSM_GROUPS = 2
SSM_STATE = 128
CONV_WIDTH = 5
CHUNK = 128
D_XBC = D_SSM + 2 * SSM_GROUPS * SSM_STATE

ATTN_HEAD_DIM = 64
ATTN_HEADS = D_MODEL // ATTN_HEAD_DIM
ATTN_KV_HEADS = 4
D_ATTN = ATTN_HEADS * ATTN_HEAD_DIM
D_KV = ATTN_KV_HEADS * ATTN_HEAD_DIM
WINDOW = 128
BLOCK = 128

D_MIX = D_SSM + D_ATTN
D_FF = 4 * D_MODEL
EPS = 1e-6

IN_SIZES = (D_SSM, D_XBC, 2 * SSM_HEADS, D_ATTN, D_KV, D_KV)
D_IN = sum(IN_SIZES)
IN_OFFSETS = tuple(int(o) for o in np.cumsum(IN_SIZES)[:-1])

kernel_name = "hybrid_ssd_swa_encoder_layer"


def rmsnorm(x, w):
    xf = x.astype(jnp.float32)
    y = xf * lax.rsqrt(jnp.mean(xf * xf, axis=-1, keepdims=True) + EPS)
    return (y * w.astype(jnp.float32)).astype(x.dtype)


def depthwise_conv_centred(u, w, b):
    c = u.shape[-1]
    out = lax.conv_general_dilated(
        u, w[:, None, :].astype(u.dtype), window_strides=(1,),
        padding=[(CONV_WIDTH // 2, CONV_WIDTH // 2)],
        dimension_numbers=("NWC", "WIO", "NWC"), feature_group_count=c)
    return out + b.astype(u.dtype)


def ssd_scan(X, dt, A, B, C):
    b, l, h, p = X.shape
    g, n = B.shape[2], B.shape[3]
    r = h // g
    c = l // CHUNK
    Xd = (X.astype(jnp.float32) * dt[..., None]).reshape(b, c, CHUNK, g, r, p)
    dA = (dt * A).reshape(b, c, CHUNK, g, r).transpose(0, 3, 4, 1, 2)
    Bc = B.astype(jnp.float32).reshape(b, c, CHUNK, g, n)
    Cc = C.astype(jnp.float32).reshape(b, c, CHUNK, g, n)
    A_cs = jnp.cumsum(dA, axis=-1)
    tri = jnp.tril(jnp.ones((CHUNK, CHUNK), dtype=bool))
    Lmat = jnp.exp(jnp.where(tri, A_cs[..., :, None] - A_cs[..., None, :], -jnp.inf))
    CB = jnp.einsum("bclgn,bcsgn->bgcls", Cc, Bc)
    Y_diag = jnp.einsum("bgrcls,bcsgrp->bclgrp", CB[:, :, None] * Lmat, Xd)
    decay_states = jnp.exp(A_cs[..., -1:] - A_cs)
    states = jnp.einsum("bcsgn,bgrcs,bcsgrp->bcgrpn", Bc, decay_states, Xd)
    chunk_decay = jnp.exp(A_cs[..., -1])

    def step(carry, inp):
        s_c, dec_c = inp
        return carry * dec_c[..., None, None] + s_c, carry

    init = jnp.zeros_like(states[:, 0])
    _, prev = lax.scan(step, init, (states.transpose(1, 0, 2, 3, 4, 5),
                                    chunk_decay.transpose(3, 0, 1, 2)))
    Y_off = jnp.einsum("bclgn,cbgrpn,bgrcl->bclgrp", Cc, prev, jnp.exp(A_cs))
    return (Y_diag + Y_off).reshape(b, l, h, p)


def ssd_mixer(z, xbc, dt_raw, conv_w, conv_b, dt_bias, a_log, d_skip, norm_w):
    b, l, _ = xbc.shape
    xbc = jax.nn.silu(depthwise_conv_centred(xbc, conv_w, conv_b))
    xs, Bm, Cm = jnp.split(xbc, (D_SSM, D_SSM + SSM_GROUPS * SSM_STATE), axis=-1)
    X = xs.reshape(b, l, SSM_HEADS, SSM_HEAD_DIM)
    Bm = Bm.reshape(b, l, SSM_GROUPS, SSM_STATE)
    Cm = Cm.reshape(b, l, SSM_GROUPS, SSM_STATE)
    dt = jax.nn.softplus(dt_raw.astype(jnp.float32).reshape(b, l, 2, SSM_HEADS)
                         + dt_bias.astype(jnp.float32))
    A = -jnp.exp(a_log.astype(jnp.float32))
    flip = lambda t: jnp.flip(t, axis=1)
    y_f = ssd_scan(X, dt[:, :, 0], A[0], Bm, Cm)
    y_b = flip(ssd_scan(flip(X), flip(dt[:, :, 1]), A[1], flip(Bm), flip(Cm)))
    y = y_f + y_b + X.astype(jnp.float32) * d_skip.astype(jnp.float32)[:, None]
    y = y.reshape(b, l, D_SSM)
    return rmsnorm(y * jax.nn.silu(z.astype(jnp.float32)), norm_w)


def window_attention(q, k, v, sink):
    b, l, _ = q.shape
    nb = l // BLOCK
    G, R, Dh = ATTN_KV_HEADS, ATTN_HEADS // ATTN_KV_HEADS, ATTN_HEAD_DIM
    qb = q.reshape(b, nb, BLOCK, G, R, Dh)
    pad = ((0, 0), (WINDOW, WINDOW), (0, 0))
    kp = jnp.pad(k, pad).reshape(b, nb + 2, BLOCK, G, Dh)
    vp = jnp.pad(v, pad).reshape(b, nb + 2, BLOCK, G, Dh)
    kb = jnp.concatenate([kp[:, :-2], kp[:, 1:-1], kp[:, 2:]], axis=2)
    vb = jnp.concatenate([vp[:, :-2], vp[:, 1:-1], vp[:, 2:]], axis=2)
    s = jnp.einsum("bnqgrd,bnkgd->bgrnqk", qb, kb).astype(jnp.float32) * (Dh ** -0.5)
    qi = jnp.arange(BLOCK)[:, None]
    kj = jnp.arange(3 * BLOCK)[None, :]
    dist = kj - BLOCK - qi
    key_abs = jnp.arange(nb)[:, None, None] * BLOCK + (kj - BLOCK)[None]
    valid = (jnp.abs(dist) <= WINDOW)[None] & (key_abs >= 0) & (key_abs < l)
    slopes = 2.0 ** (-8.0 * jnp.arange(1, ATTN_HEADS + 1, dtype=jnp.float32) / ATTN_HEADS)
    slopes = slopes.reshape(G, R)[:, :, None, None, None]
    s = jnp.where(valid, s - slopes * jnp.abs(dist).astype(jnp.float32), -jnp.inf)
    sk = sink.astype(jnp.float32).reshape(G, R)[:, :, None, None, None]
    m = jnp.maximum(jnp.max(s, axis=-1, keepdims=True), sk)
    p = jnp.exp(s - m)
    probs = p / (jnp.sum(p, axis=-1, keepdims=True) + jnp.exp(sk - m))
    o = jnp.einsum("bgrnqk,bnkgd->bnqgrd", probs.astype(v.dtype), vb)
    return o.reshape(b, l, D_ATTN)


def setup_inputs(seed: int = 0) -> dict:
    key = jax.random.key(seed)
    ks = jax.random.split(key, 18)
    f32 = jnp.float32
    nrm = lambda k, shape, scale: jax.random.normal(k, shape, f32) * scale
    gain = lambda k, shape: 1.0 + 0.05 * jax.random.normal(k, shape, f32)
    dt0 = jnp.exp(jax.random.uniform(ks[5], (DEPTH, 2, SSM_HEADS), f32,
                                     jnp.log(1e-3), jnp.log(1e-1)))
    dt_bias = dt0 + jnp.log(-jnp.expm1(-dt0))
    a_log = jnp.log(jax.random.uniform(ks[6], (DEPTH, 2, SSM_HEADS), f32, 1.0, 16.0))
    return {
        "x": nrm(ks[0], (BATCH, SEQ, D_MODEL), 1.0),
        "norm_mix_pre": gain(ks[1], (DEPTH, D_MODEL)),
        "w_in": nrm(ks[2], (DEPTH, D_MODEL, D_IN), D_MODEL ** -0.5),
        "conv_w": nrm(ks[3], (DEPTH, CONV_WIDTH, D_XBC), CONV_WIDTH ** -0.5),
        "conv_b": nrm(ks[4], (DEPTH, D_XBC), 0.01),
        "dt_bias": dt_bias,
        "a_log": a_log,
        "d_skip": gain(ks[7], (DEPTH, SSM_HEADS)),
        "ssm_norm": gain(ks[8], (DEPTH, D_SSM)),
        "attn_sink": nrm(ks[9], (DEPTH, ATTN_HEADS), 0.5),
        "w_out": nrm(ks[10], (DEPTH, D_MIX, D_MODEL), D_MIX ** -0.5),
        "norm_mix_post": gain(ks[11], (DEPTH, D_MODEL)),
        "norm_mlp_pre": gain(ks[12], (DEPTH, D_MODEL)),
        "w_up": nrm(ks[13], (DEPTH, D_MODEL, D_FF), D_MODEL ** -0.5),
        "w_down": nrm(ks[14], (DEPTH, D_FF, D_MODEL), D_FF ** -0.5),
        "norm_mlp_post": gain(ks[15], (DEPTH, D_MODEL)),
    }


def reference(x, norm_mix_pre, w_in, conv_w, conv_b, dt_bias, a_log, d_skip, ssm_norm,
              attn_sink, w_out, norm_mix_post, norm_mlp_pre, w_up, w_down, norm_mlp_post):
    for i in range(DEPTH):
        h = rmsnorm(x, norm_mix_pre[i])
        proj = h @ w_in[i]
        z, xbc, dt_raw, q, k, v = jnp.split(proj, IN_OFFSETS, axis=-1)
        y_ssm = ssd_mixer(z, xbc, dt_raw, conv_w[i], conv_b[i], dt_bias[i], a_log[i],
                          d_skip[i], ssm_norm[i])
        y_attn = window_attention(q, k, v, attn_sink[i])
        mix = jnp.concatenate([y_ssm.astype(x.dtype), y_attn.astype(x.dtype)], axis=-1) @ w_out[i]
        x = x + rmsnorm(mix, norm_mix_post[i])
        h = rmsnorm(x, norm_mlp_pre[i])
        f = jnp.square(jax.nn.relu(h @ w_up[i])) @ w_down[i]
        x = x + rmsnorm(f, norm_mlp_post[i])
    return x
```

```python
import os
from contextlib import ExitStack
import numpy as np
import ml_dtypes
import concourse.bass as bass
import concourse.mybir as mybir
from concourse.bass_utils import run_bass_kernel_spmd

F32 = mybir.dt.float32
BF16 = mybir.dt.bfloat16
AF = mybir.ActivationFunctionType
ALU = mybir.AluOpType
NPBF = ml_dtypes.bfloat16

D = 1024
NLOC = 8192
NOWN = 4096
WIN_COLS = 4896
EPS = 1e-6
DEBUG = {}

ENGS = ("pe", "act", "dve", "pool", "sp")


class Op:
    __slots__ = ("eng", "fn", "reads", "writes", "dkey", "waits", "sig", "dcount")

    def __init__(self, eng, fn, reads, writes, dkey):
        self.eng, self.fn, self.reads, self.writes, self.dkey = eng, fn, reads, writes, dkey
        self.waits = []
        self.sig = None
        self.dcount = None


class Prog:
    def __init__(self, nc):
        self.nc = nc
        self.ops = []

    def add(self, eng, fn, reads=(), writes=(), dkey=None):
        self.ops.append(Op(eng, fn, tuple(reads), tuple(writes), dkey))

    def finalize(self, stack):
        nc, ops = self.nc, self.ops
        last_w, readers, deps_of = {}, {}, []
        canon = lambda r: r[:3] if r.startswith("bk") else r
        for i, op in enumerate(ops):
            deps = set()
            rds = {canon(r) for r in op.reads}
            wrs = {canon(r) for r in op.writes}
            for r in rds:
                w = last_w.get(r)
                if w is not None:
                    deps.add(w)
                if r.startswith("bk"):
                    deps.update(j for j in readers.get(r, ()) if ops[j].eng != op.eng)
            for r in wrs:
                w = last_w.get(r)
                if w is not None:
                    deps.add(w)
                deps.update(readers.get(r, ()))
            deps.discard(i)
            for r in rds:
                readers.setdefault(r, []).append(i)
            for r in wrs:
                last_w[r] = i
                readers[r] = []
            deps = {d for d in deps if not (ops[d].eng == "pe" and op.eng == "pe"
                                            and ops[d].dkey is None and op.dkey is None)}
            deps_of.append(deps)
        needed = set()
        for deps in deps_of:
            for d in deps:
                if ops[d].dkey is None:
                    needed.add(d)
        cnt, dcnt = {}, {}
        for i, op in enumerate(ops):
            if op.dkey is not None:
                dcnt[op.dkey] = dcnt.get(op.dkey, 0) + 1
                op.dcount = dcnt[op.dkey]
            elif i in needed:
                cnt[op.eng] = cnt.get(op.eng, 0) + 1
                op.sig = cnt[op.eng]
        self.sems = {}
        for e in ENGS:
            self.sems[("c", e)] = stack.enter_context(nc.semaphore("s_" + e))
        for k in dcnt:
            self.sems[("d", k)] = stack.enter_context(nc.semaphore("d_" + str(k)))
        waited = {}
        for i, op in enumerate(ops):
            req = {}
            for d in deps_of[i]:
                p = ops[d]
                if p.dkey is not None:
                    key, val = ("d", p.dkey), 16 * p.dcount
                else:
                    key, val = ("c", p.eng), p.sig
                if val > req.get(key, 0):
                    req[key] = val
            for key, val in req.items():
                if waited.get((op.eng, key), 0) < val:
                    waited[(op.eng, key)] = val
                    op.waits.append((key, val))

    def emit(self):
        nc, ops, sems = self.nc, self.ops, self.sems

        def run(engname, e):
            for op in ops:
                if op.eng != engname:
                    continue
                for key, val in op.waits:
                    e.wait_ge(sems[key], val)
                if op.fn is None:
                    continue
                ins = op.fn(e)
                if op.dkey is not None:
                    ins.then_inc(sems[("d", op.dkey)], 16)
                elif op.sig is not None:
                    ins.then_inc(sems[("c", op.eng)], 1)

        with nc.Block() as block:
            @block.sync
            def _(e):
                run("sp", e)

            @block.scalar
            def _(e):
                run("act", e)

            @block.vector
            def _(e):
                run("dve", e)

            @block.gpsimd
            def _(e):
                run("pool", e)

            @block.tensor
            def _(e):
                run("pe", e)


def build(n_groups2=8, n_groups1=16, debug=None, stop=None):
    INTERLEAVE = bool(int(os.environ.get('MK_INTERLEAVE', '1')))
    QX = os.environ.get('MK_QX', 'sp')
    PIPE2 = (stop is None) and bool(int(os.environ.get('MK_PIPE2', '1')))
    nc = bass.Bass("TRN2", target_bir_lowering=False)
    st = ExitStack()
    P = Prog(nc)
    din = lambda name, shape, dt=F32: nc.dram_tensor(name, list(shape), dt, kind="ExternalInput").ap()
    x_d = din("x", [NLOC + 2, D])
    win_d = din("w_in", [D, WIN_COLS])
    wout_d = din("w_out", [2048, D])
    wup_d = din("w_up", [D, 4096])
    wdn_d = din("w_down", [4096, D])
    gains_d = din("gains", [128, 5, D])
    dskip_d = din("dskip", [128, D])
    convw_d = din("convw", [128, 12, 5])
    convb_d = din("convb", [128, 12])
    dtb_d = din("dtb", [128, 32])
    alog_d = din("alog", [128, 32])
    sink_d = din("sink", [128, 8])
    cbf_d = din("cbf", [128, 6, 128], BF16)
    cf32_d = din("cf32", [128, 3, 128])
    abias_d = din("abias", [128, 4, 3, 512], BF16)
    out_d = nc.dram_tensor("out", [NOWN, D], F32, kind="ExternalOutput").ap()
    win_s = nc.dram_tensor("win_s", [D, WIN_COLS], BF16).ap()
    wout_s = nc.dram_tensor("wout_s", [2048, D], BF16).ap()
    wup_s = nc.dram_tensor("wup_s", [D, 4096], BF16).ap()
    wdn_s = nc.dram_tensor("wdn_s", [4096, D], BF16).ap()
    pb_s = nc.dram_tensor("pb_s", [32, 128, D], BF16).ap()
    dbg_outs = {}

    def sb(name, shape, dt):
        return st.enter_context(nc.sbuf_tensor("sb_" + name, list(shape), dt))

    cbf = sb("cbf", [128, 6, 128], BF16)
    cf32 = sb("cf32", [128, 3, 128], F32)
    abias = sb("abias", [128, 4, 3, 512], BF16)
    gains = sb("gains", [128, 5, D], BF16)
    dskip = sb("dskipb", [128, D], BF16)
    cdiag = sb("cdiag", [128, 12, 5, 128], BF16)
    convw = sb("convw", [128, 12, 5], F32)
    convb = sb("convb", [128, 12], F32)
    dtb = sb("dtb", [128, 32], F32)
    negA = sb("negA", [128, 32], F32)
    esk = sb("esk", [128, 8], F32)
    EPSB = sb("epsb", [128, 1], F32)
    MHALF = sb("mhalf", [128, 1], F32)
    IDENT, UU, UB, SU, SG, ONES = (cbf[:, i, :] for i in range(6))
    U_F, SU_F, ONES_F = (cf32[:, i, :] for i in range(3))
    G_MIXPRE, G_SSM, G_MIXPOST, G_MLPPRE, G_MLPPOST = (gains[:, i, :] for i in range(5))
    xin = [sb(f"xin{i}", [128, D], F32) for i in range(2)]
    hT = sb("hT", [128, 8, 642], BF16)
    hb = [sb(f"hb{i}", [128, D], BF16) for i in range(1)]
    stat = sb("stat", [128, 8], F32)
    u_bf = [sb(f"u_bf{i}", [128, 516], BF16) for i in range(2)]
    ucarry = sb("ucarry", [128, 12, 2], BF16)
    xc = sb("xc", [128, 12, 512], BF16)
    qT = sb("qT", [128, 8, 512], BF16)
    kT = sb("kT", [128, 8, 768], BF16)
    vv = sb("vv", [128, 6, 256], BF16)
    zs = sb("zs", [128, 4, D], BF16)
    dtraw = sb("dtraw", [128, 4, 32], F32)
    Xtok = sb("Xtok", [128, D], BF16)
    Xdf = sb("Xdf", [128, D], BF16)
    Xdb = sb("Xdb", [128, D], BF16)
    XwS = sb("XwS", [128, D], BF16)
    XDs = sb("XDs", [128, D], BF16)
    Btok = sb("Btok", [128, 256], BF16)
    rhs_all0 = sb("rhs_all", [128, 16, 128], BF16)
    Ering = [sb(f"Er{i}", [128, 4, 128], BF16) for i in range(5)]
    CBm = sb("CBm", [128, 2, 2, 128], BF16)
    smb = sb("smb", [128, 128], BF16)
    sm = sb("sm", [128, 9, 128], F32)
    stt = sb("stt", [128, D], F32)
    prevf = [sb(f"prevf{i}", [128, D], BF16) for i in range(2)]
    prevb = [sb(f"prevb{i}", [128, D], BF16) for i in range(2)]
    PT = sb("PT", [128, 3, 512], BF16)
    rden = sb("rden", [128, 256], F32)
    ycatT = sb("ycatT", [128, 2, 16, 128], BF16)
    gbuf = sb("gbuf", [128, D], F32)
    x1 = sb("x1", [128, 2, D], F32)
    h2T = sb("h2T", [128, 8, 256], BF16)
    rhs_alls = [(rhs_all0[:], "rhs_all"), (h2T[:].rearrange("p a (b c) -> p (a b) c", b=2), "h2T")]
    fring = [sb(f"fr{i}", [128, 256], BF16) for i in range(4)]
    NRING = 4
    wring = [sb(f"wr{i}", [128, 4096], BF16) for i in range(NRING)]
    banks = [st.enter_context(nc.psum_tensor(f"bk{i}", [128, 512], F32)) for i in range(8)]
    print("sbuf bytes remaining:", nc.sbuf_bytes_remaining)

    def bk(i):
        return [f"bk{i}a", f"bk{i}b"]
    BK4 = ["bk4.uh", "bk4.bt", "bk4.cb"]
    ACCB = BK4 + bk(5) + bk(6) + bk(7)

    cnt = {"pd": 0, "pA": 0, "xin": 0, "u": 0, "er": 0, "fr": 0, "pu": 0, "hb": 0, "ev": 0}

    def dma(eng, out, in_, reads, writes, key):
        P.add(eng, lambda e: e.dma_start(out=out, in_=in_), reads=reads, writes=writes, dkey=key)

    wseq = []
    wstate = {"issued": 0, "used": 0}

    def w_issue_upto(n):
        while wstate["issued"] < min(n, len(wseq)):
            i = wstate["issued"]
            src, shape, sres = wseq[i]
            slot = i % NRING
            nel = shape[0] * shape[1]
            dst = wring[slot][:, 0:nel].rearrange("p (a b) -> p a b", a=shape[0])
            dma("sp", dst, src, [sres], [f"wr{slot}"], f"wr{slot}")
            wstate["issued"] += 1

    def w_use(shape):
        i = wstate["used"]
        wstate["used"] += 1
        w_issue_upto(i + NRING - 1)
        slot = i % NRING
        assert tuple(wseq[i][1]) == tuple(shape), (i, wseq[i][1], shape)
        nel = shape[0] * shape[1]
        v = wring[slot][:, 0:nel].rearrange("p (a b) -> p a b", a=shape[0])
        return v, f"wr{slot}"

    def win_piece(c0, ncols):
        res = "win_p1" if (c0 + ncols <= 1280 or c0 >= 4608) else "win_rest"
        return (win_s[:, c0:c0 + ncols].rearrange("(kt p) c -> p kt c", p=128), (8, ncols), res)

    def seq_phase1_group():
        return [win_piece(0, 512), win_piece(512, 512), win_piece(1024, 256), win_piece(4608, 288)]

    def seq_stageA():
        s = [win_piece(0, 512), win_piece(512, 512)]
        s.append((win_s[:, 1024:1536].rearrange("(kt p) c -> p kt c", p=128), (8, 512), "win_rest"))
        s += [win_piece(512 * i, 512) for i in range(3, 7)]
        s += [win_piece(3584, 512), win_piece(4096, 512), win_piece(4608, 288)]
        return s

    def seq_outproj():
        return [(wout_s[512 * i:512 * i + 512, :].rearrange("(kt p) c -> p kt c", p=128), (4, 1024), "wout_s") for i in (2, 3, 0, 1)]

    def seq_ffn():
        s = []
        for u in range(8):
            s.append((wup_s[:, 512 * u:512 * u + 512].rearrange("(kt p) c -> p kt c", p=128), (8, 512), "wup_s"))
            s.append((wdn_s[512 * u:512 * u + 512, :].rearrange("(j p) c -> p j c", p=128), (4, 1024), "wdn_s"))
        return s

    def seq_phase2_all():
        s = []
        if n_groups2 == 0:
            return s
        s += seq_stageA()
        for g in range(n_groups2):
            if PIPE2:
                s += seq_outproj() + seq_ffn() + seq_outproj()
                if g + 1 < n_groups2:
                    s += seq_stageA()
                s += seq_ffn()
            else:
                s += seq_outproj() + seq_ffn() + seq_outproj() + seq_ffn()
                if g + 1 < n_groups2:
                    s += seq_stageA()
        return s

    for _ in range(n_groups1):
        wseq.extend(seq_phase1_group())
    wseq.extend(seq_phase2_all())

    def next_pA():
        i = cnt["pA"] % 2
        cnt["pA"] += 1
        return banks[i], bk(i)

    def evac_copy(out, in_, reads, writes):
        cnt["ev"] += 1
        if cnt["ev"] % 2:
            P.add("dve", lambda e: e.tensor_copy(out=out, in_=in_), reads=reads, writes=writes)
        else:
            P.add("act", lambda e: e.activation(out=out, in_=in_, func=AF.Copy), reads=reads, writes=writes)

    def norm_s1(xsrc, xres, nrows, gain, sidx, use_pool=False):
        s0 = stat[0:nrows, sidx:sidx + 1]
        hbb, hres = hb[0], "hb0"
        P.add("act", lambda e: e.activation(out=hbb[0:nrows, :], in_=xsrc, func=AF.Square, accum_out=s0),
              reads=[xres], writes=[hres, f"stat{sidx}"])
        if use_pool:
            P.add("pool", lambda e: e.tensor_scalar(out=s0, in0=s0, scalar1=1.0 / D, scalar2=EPS, op0=ALU.mult, op1=ALU.add),
                  reads=[f"stat{sidx}"], writes=[f"stat{sidx}"])
            P.add("pool", lambda e: e.tensor_tensor(out=s0, in0=s0, in1=MHALF[0:nrows, :], op=ALU.pow),
                  reads=[f"stat{sidx}", "mhalf"], writes=[f"stat{sidx}"])
        else:
            P.add("act", lambda e: e.activation(out=s0, in_=s0, func=AF.Ln, scale=1.0 / D, bias=EPSB[0:nrows, :]),
                  reads=[f"stat{sidx}"], writes=[f"stat{sidx}"])
            P.add("act", lambda e: e.activation(out=s0, in_=s0, func=AF.Exp, scale=-0.5),
                  reads=[f"stat{sidx}"], writes=[f"stat{sidx}"])
        P.add("dve", lambda e: e.scalar_tensor_tensor(out=hbb[0:nrows, :], in0=xsrc, scalar=s0, in1=gain[0:nrows, :],
                                                      op0=ALU.mult, op1=ALU.mult),
              reads=[xres, f"stat{sidx}"], writes=[hres])

    def norm_s2(nrows, dstT, dres, off):
        hbb, hres = hb[0], "hb0"
        pT = banks[3][:].bitcast(BF16)

        def tr(e):
            for k in range(8):
                ins = e.transpose(out=pT[:, 128 * k:128 * k + nrows], in_=hbb[0:nrows, 128 * k:128 * k + 128],
                                  identity=IDENT[0:nrows, 0:nrows])
            return ins
        P.add("pe", tr, reads=[hres], writes=["bk3"])
        src = pT.rearrange("p (k n) -> p k n", k=8)[:, :, 0:nrows]
        evac_copy(dstT[:, :, off:off + nrows], src, ["bk3"], [dres])

    def norm_T(xsrc, xres, nrows, gain, dstT, dres, off, sidx):
        norm_s1(xsrc, xres, nrows, gain, sidx)
        norm_s2(nrows, dstT, dres, off)

    def staged(units2):
        out = []
        for i in range(len(units2) + 1):
            def u(i=i):
                if i > 0:
                    units2[i - 1][1]()
                if i < len(units2):
                    units2[i][0]()
            out.append(u)
        return out

    def setup():
        P.add("pool", lambda e: e.memset(EPSB[:], EPS), writes=["epsb"])
        P.add("pool", lambda e: e.memset(MHALF[:], -0.5), writes=["mhalf"])
        for i in range(8):
            r = slice(128 * i, 128 * i + 128)
            dma("pool", win_s[r, 0:1280], win_d[r, 0:1280], [], ["win_p1"], "cast_p1")
            dma("pool", win_s[r, 4608:4896], win_d[r, 4608:4896], [], ["win_p1"], "cast_p1")
        dma("sp", cbf[:], cbf_d, [], ["cbf"], "c_cbf")
        dma("sp", cf32[:], cf32_d, [], ["cf32"], "c_cf32")
        dma("sp", convw[:], convw_d, [], ["convw"], "c_convw")
        dma("sp", convb[:], convb_d, [], ["convb"], "c_convb")
        dma("sp", dtb[:], dtb_d, [], ["dtb"], "c_dtb")
        dma("sp", negA[:], alog_d, [], ["negA"], "c_negA")
        dma("sp", esk[:], sink_d, [], ["esk"], "c_esk")
        dma("sp", abias[:], abias_d, [], ["abias"], "c_abias")
        for i in range(5):
            dma("sp", gbuf[:], gains_d[:, i, :], [], ["gbuf"], "gf")
            P.add("dve", lambda e, i=i: e.tensor_copy(out=gains[:, i, :], in_=gbuf[:]), reads=["gbuf"], writes=["gains"])
        dma("sp", gbuf[:], dskip_d, [], ["gbuf"], "gf")
        P.add("dve", lambda e: e.tensor_copy(out=dskip[:], in_=gbuf[:]), reads=["gbuf"], writes=["dskip"])
        P.add("act", lambda e: e.activation(out=negA[:], in_=negA[:], func=AF.Exp), reads=["negA"], writes=["negA"])
        P.add("dve", lambda e: e.tensor_scalar_mul(out=negA[:], in0=negA[:], scalar1=-1.0), reads=["negA"], writes=["negA"])
        P.add("act", lambda e: e.activation(out=esk[:], in_=esk[:], func=AF.Exp), reads=["esk"], writes=["esk"])
        for t in range(12):
            for k in range(5):
                P.add("pool", lambda e, t=t, k=k: e.tensor_scalar_mul(out=cdiag[:, t, k, :], in0=IDENT,
                                                                      scalar1=convw[:, t, k:k + 1]),
                      reads=["cbf", "convw"], writes=["cdiag"])
        P.add("pool", lambda e: e.memset(ucarry[:], 0.0), writes=[f"uc{t}" for t in range(12)])
        P.add("pool", lambda e: e.memset(stt[:], 0.0), writes=["stt"])
        for i in range(8):
            r = slice(128 * i, 128 * i + 128)
            cast_jobs.append((win_s[r, 1280:4608], win_d[r, 1280:4608], "win_rest", "cast_rest"))
        for i in range(16):
            cast_jobs.append((wout_s[128 * i:128 * i + 128, :], wout_d[128 * i:128 * i + 128, :], "wout_s", "cast_out"))
        for i in range(8):
            cast_jobs.append((wup_s[128 * i:128 * i + 128, :], wup_d[128 * i:128 * i + 128, :], "wup_s", "cast_up"))
        for i in range(32):
            cast_jobs.append((wdn_s[128 * i:128 * i + 128, :], wdn_d[128 * i:128 * i + 128, :], "wdn_s", "cast_dn"))

    cast_jobs = []

    def emit_casts(n):
        for _ in range(min(n, len(cast_jobs))):
            dst, src, res, key = cast_jobs.pop(0)
            dma("pool", dst, src, [], [res], key)

    def load_x(rows, nrows):
        i = cnt["xin"] % 2
        cnt["xin"] += 1
        dma(QX, xin[i][0:nrows, :], x_d[rows, :], [], [f"xin{i}"], f"xin{i}")
        return xin[i][0:nrows, :], f"xin{i}"

    def tile_p1(t, wv, wres, wcol, main_lo, halo_lo, halo_side, hT=hT, hres="hT"):
        bank, bres = next_pA()

        def mm(e):
            for k in range(8):
                ins = e.matmul(bank[:, :], lhsT=wv[:, k, wcol:wcol + 128], rhs=hT[:, k, main_lo:main_lo + 512],
                               start=(k == 0), stop=(k == 7))
            return ins
        P.add("pe", mm, reads=[wres, hres], writes=bres)
        hp = banks[4][:, 224 + 2 * t:226 + 2 * t]

        def mmh(e):
            for k in range(8):
                ins = e.matmul(hp, lhsT=wv[:, k, wcol:wcol + 128], rhs=hT[:, k, halo_lo:halo_lo + 2],
                               start=(k == 0), stop=(k == 7))
            return ins
        P.add("pe", mmh, reads=[wres, hres], writes=["bk4.uh"])
        ui = cnt["u"] % 2
        cnt["u"] += 1
        ub, ures = u_bf[ui], f"u_bf{ui}"
        P.add("act", lambda e: e.activation(out=ub[:, 2:514], in_=bank[:, :], func=AF.Copy), reads=bres, writes=[ures + "m"])
        if halo_side == "R":
            P.add("dve", lambda e: e.tensor_copy(out=ub[:, 514:516], in_=hp), reads=["bk4.uh"], writes=[ures + "r"])
            P.add("dve", lambda e: e.tensor_copy(out=ub[:, 0:2], in_=ucarry[:, t, :]), reads=[f"uc{t}"], writes=[ures + "l"])
            P.add("dve", lambda e: e.tensor_copy(out=ucarry[:, t, :], in_=ub[:, 512:514]), reads=[ures + "m"], writes=[f"uc{t}"])
        else:
            P.add("dve", lambda e: e.tensor_copy(out=ub[:, 0:2], in_=hp), reads=["bk4.uh"], writes=[ures + "l"])
            P.add("dve", lambda e: e.tensor_copy(out=ub[:, 514:516], in_=ucarry[:, t, :]), reads=[f"uc{t}"], writes=[ures + "r"])
            P.add("dve", lambda e: e.tensor_copy(out=ucarry[:, t, :], in_=ub[:, 2:4]), reads=[ures + "m"], writes=[f"uc{t}"])
        return t, ub, [ures + "m", ures + "l", ures + "r"]

    def xct(xb, t):
        if xb == 0:
            return xc[:, t, :], f"xc{t}"
        if t < 8:
            return qT[:, t, :], f"xq{t}"
        return ycatT[:, 0, 4 * (t - 8):4 * (t - 8) + 4, :].rearrange("p a b -> p (a b)"), f"xq{t}"

    def tile_p2(t, ub, ures, xb=0):
        def cv(e):
            for k in range(5):
                ins = e.matmul(banks[2][:, :], lhsT=cdiag[:, t, k, :], rhs=ub[:, k:k + 512], start=(k == 0), stop=(k == 4))
            return ins
        P.add("pe", cv, reads=ures + ["cdiag"], writes=bk(2))
        xo, xres_ = xct(xb, t)
        P.add("act", lambda e: e.activation(out=xo, in_=banks[2][:, :], func=AF.Silu, bias=convb[:, t:t + 1]),
              reads=bk(2) + ["convb"], writes=[xres_])

    def dt_prep_group(fwd):
        dt_prep_a()
        dt_prep_b(fwd)

    def dt_prep_a():
        S = lambda i: sm[:, i, :]
        S3 = lambda i: sm[:, i, :].rearrange("p (c k) -> p c k", c=4)
        bc4 = lambda ap: ap[:, None, :].broadcast_to([128, 4, 32])
        P.add("dve", lambda e: e.tensor_tensor(out=S3(0), in0=dtraw[:], in1=bc4(dtb[:]), op=ALU.add),
              reads=["dtraw", "dtb"], writes=["sm0"])
        P.add("act", lambda e: e.activation(out=S(0), in_=S(0), func=AF.Exp), reads=["sm0"], writes=["sm0"])
        P.add("act", lambda e: e.activation(out=S(1), in_=S(0), func=AF.Ln, bias=1.0), reads=["sm0"], writes=["sm1"])
        P.add("dve", lambda e: e.tensor_tensor(out=S3(2), in0=S3(1), in1=bc4(negA[:]), op=ALU.mult), reads=["sm1", "negA"], writes=["sm2"])
        P.add("act", lambda e: e.activation(out=S(3), in_=S(1), func=AF.Ln), reads=["sm1"], writes=["sm3"])
        P.add("dve", lambda e: e.tensor_copy(out=smb[:], in_=S(2)), reads=["sm2"], writes=["smb"])

    def dt_prep_b(fwd):
        S = lambda i: sm[:, i, :]
        S3 = lambda i: sm[:, i, :].rearrange("p (c k) -> p c k", c=4)
        bank, bres = next_pA()

        def mm(e):
            e.matmul(bank[:, 0:128], lhsT=U_F, rhs=S(2), start=True, stop=True)
            e.matmul(bank[:, 128:256], lhsT=SU_F, rhs=S(2), start=True, stop=True)
            return e.matmul(bank[:, 256:384], lhsT=ONES_F, rhs=S(2), start=True, stop=True)
        P.add("pe", mm, reads=["sm2", "cf32"], writes=bres)
        P.add("dve", lambda e: e.tensor_copy(out=sm[:, 4:7, :], in_=bank[:, 0:384].rearrange("p (a b) -> p a b", a=3)),
              reads=bres, writes=["sm456"])
        if fwd:
            sl = lambda i: S3(i)[:, :, 0:16]
            P.add("dve", lambda e: e.tensor_tensor(out=sl(7), in0=sl(6), in1=sl(4), op=ALU.subtract), reads=["sm456"], writes=["sm7"])
            P.add("dve", lambda e: e.tensor_tensor(out=sl(7), in0=sl(7), in1=sl(3), op=ALU.add), reads=["sm7", "sm3"], writes=["sm7"])
        else:
            sl = lambda i: S3(i)[:, :, 16:32]
            P.add("dve", lambda e: e.tensor_tensor(out=sl(7), in0=sl(5), in1=sl(3), op=ALU.add), reads=["sm456", "sm3"], writes=["sm7"])
        P.add("act", lambda e: e.activation(out=sl(7), in_=sl(7), func=AF.Exp), reads=["sm7"], writes=["sm7"])
        P.add("act", lambda e: e.activation(out=sl(8), in_=sl(6), func=AF.Exp), reads=["sm456"], writes=["sm8"])

    def tok_transposes(ci, xb=0):
        cols = slice(128 * ci, 128 * ci + 128)
        pT = banks[3][:].bitcast(BF16)
        xt = [xct(xb, t) for t in range(10)]

        def tr(e):
            for t in range(8):
                ins = e.transpose(out=pT[:, 128 * t:128 * t + 128], in_=xt[t][0][:, cols], identity=IDENT)
            return ins
        P.add("pe", tr, reads=[xt[t][1] for t in range(8)], writes=["bk3"])
        P.add("act", lambda e: e.activation(out=Xtok[:], in_=pT[:, :], func=AF.Copy), reads=["bk3"], writes=["Xtok"])
        pB = banks[4][:, 0:128].bitcast(BF16)

        def trb(e):
            for t in range(2):
                ins = e.transpose(out=pB[:, 128 * t:128 * t + 128], in_=xt[8 + t][0][:, cols], identity=IDENT)
            return ins
        P.add("pe", trb, reads=[xt[8][1], xt[9][1]], writes=["bk4.bt"])
        P.add("dve", lambda e: e.tensor_copy(out=Btok[:], in_=pB[:, :]), reads=["bk4.bt"], writes=["Btok"])

    def bc_h(ap16):
        return ap16[:, :, None].broadcast_to([128, 16, 64])

    def v3(ap):
        return ap.rearrange("p (h d) -> p h d", h=16)

    def smc(slot, ci, di):
        return sm[:, slot, 32 * ci + 16 * di:32 * ci + 16 * di + 16]

    def state_update(ci, di, S_banks):
        dec = smc(8, ci, di)
        P.add("dve", lambda e: e.tensor_tensor(out=v3(stt[:]), in0=v3(stt[:]), in1=bc_h(dec), op=ALU.mult),
              reads=["stt", "sm8"], writes=["stt"])
        for gi in range(2):
            P.add("dve", lambda e, gi=gi: e.tensor_tensor(out=stt[:, 512 * gi:512 * gi + 512], in0=stt[:, 512 * gi:512 * gi + 512],
                                                          in1=S_banks[gi][0][:, :], op=ALU.add),
                  reads=["stt"] + S_banks[gi][1], writes=["stt"])

    def state_matmuls(fixed=None):
        Sb = []
        for gi in range(2):
            bank, bres = next_pA() if fixed is None else (banks[fixed[gi]], bk(fixed[gi]))
            P.add("pe", lambda e, gi=gi, bank=bank: e.matmul(bank[:, :], lhsT=Btok[:, 128 * gi:128 * gi + 128],
                                                           rhs=XwS[:, 512 * gi:512 * gi + 512], start=True, stop=True),
                  reads=["Btok", "XwS"], writes=bres)
            Sb.append((bank, bres))
        return Sb

    hTs = [(hT, "hT"), (kT[:, :, 0:642], "kT")]

    def p1_norm_units(g, hbuf, hres):
        u2 = []
        for ci in range(4):
            def s1(ci=ci):
                c = 4 * g + ci
                xs, xr = load_x(slice(2 + 128 * c, 2 + 128 * c + 128), 128)
                norm_s1(xs, xr, 128, G_MIXPRE, ci % 2, use_pool=True)
            u2.append((s1, lambda ci=ci: norm_s2(128, hbuf, hres, 2 + 128 * ci)))

        def s1b():
            xs, xr = load_x(slice(512 * g, 512 * g + 2), 2)
            norm_s1(xs, xr, 2, G_MIXPRE, 2, use_pool=True)
        u2.append((s1b, lambda: norm_s2(2, hbuf, hres, 0)))
        return staged(u2)

    def p1_tile_units(g, hbuf, hres, xb):
        units = []
        st_ = {"pend": None, "wv": None}

        def tile(t):
            if t == 0 or t == 4:
                st_["wv"] = w_use((8, 512))
            elif t == 8:
                st_["wv"] = w_use((8, 256))
            wv, wr_ = st_["wv"]
            cur = tile_p1(t, wv, wr_, 128 * (t % 4), 2, 0, "L", hbuf, hres)
            if st_["pend"] is not None:
                tile_p2(*st_["pend"], xb=xb)
            st_["pend"] = cur
        for t in range(10):
            units.append(lambda t=t: tile(t))

        def fin():
            wdt, rdt = w_use((8, 288))
            bank, bres = next_pA()

            def mm(e):
                for ci in range(4):
                    for k in range(8):
                        ins = e.matmul(bank[:, 32 * ci:32 * ci + 32], lhsT=hbuf[:, k, 2 + 128 * ci:2 + 128 * ci + 128], rhs=wdt[:, k, 256:288],
                                       start=(ci == 0 and k == 0), stop=(k == 7), skip_group_check=True)
                return ins
            P.add("pe", mm, reads=[hres, rdt], writes=bres)
            tile_p2(*st_["pend"], xb=xb)
            P.add("dve", lambda e: e.tensor_copy(out=dtraw[:].rearrange("p c k -> p (c k)"), in_=bank[:, 0:128]), reads=bres, writes=["dtraw"])
            dt_prep_a()
        units.append(fin)
        return units

    def p1_chunk_units(g, xb):
        Sb_of = {}

        def cu1(ci):
            tok_transposes(ci, xb)
            P.add("dve", lambda e: e.tensor_tensor(out=v3(XwS[:]), in0=v3(Xtok[:]), in1=bc_h(smc(7, ci, 1)), op=ALU.mult),
                  reads=["Xtok", "sm7"], writes=["XwS"])

        def cu2(ci):
            c = 4 * g + ci
            Sb_of[ci] = state_matmuls(fixed=(5, 6))
            if c < 32:
                pi = c % 2
                P.add("act", lambda e: e.activation(out=prevb[pi][:], in_=stt[:], func=AF.Copy),
                      reads=["stt"], writes=[f"prevb{pi}"])


        def cu3(ci):
            c = 4 * g + ci
            if c < 32:
                pi = c % 2
                dma(QX, pb_s[c], prevb[pi][:], [f"prevb{pi}"], [f"pb{c}"], f"pbw{pi}")
            state_update(ci, 1, Sb_of[ci])
        order = (3, 2, 1, 0)
        units = []
        for i in range(6):
            def u(i=i):
                if 0 <= i - 2 < 4:
                    cu3(order[i - 2])
                if 0 <= i - 1 < 4:
                    cu2(order[i - 1])
                if i < 4:
                    cu1(order[i])
            units.append(u)
        return units

    def phase1_all():
        gs = list(range(n_groups1 - 1, -1, -1))
        if not gs:
            return
        for u in p1_norm_units(gs[0], *hTs[0]):
            u()
        prev_chunks = []
        pending_b = False
        for idx, g in enumerate(gs):
            A = p1_tile_units(g, *hTs[idx % 2], idx % 2)
            B = p1_norm_units(gs[idx + 1], *hTs[(idx + 1) % 2]) if idx + 1 < len(gs) else []
            C = prev_chunks
            bslot = {0: 0, 2: 1, 4: 2, 6: 3, 8: 4, 10: 5}
            cslot = {0: 0, 1: 1, 3: 2, 5: 3, 7: 4, 9: 5}
            for i in range(len(A)):
                A[i]()
                if i == 0 and pending_b:
                    dt_prep_b(False)
                    pending_b = False
                if i in (1, 5, 9):
                    emit_casts(1)
                if i in cslot and cslot[i] < len(C):
                    C[cslot[i]]()
                if i in bslot and bslot[i] < len(B):
                    B[bslot[i]]()
            assert len(C) <= 6 and len(B) <= 6 and len(A) == 11
            prev_chunks = p1_chunk_units(g, idx % 2)
            pending_b = True
        if pending_b:
            dt_prep_b(False)
        for u in prev_chunks:
            u()
        fence = [f"xq{t}" for t in range(10)] + ["qT", "ycatT0", "ycatT1", "kT", "hT"] + [f"xc{t}" for t in range(12)]
        P.add("pool", lambda e: e.memset(stt[:], 0.0), reads=["stt"], writes=["stt"] + fence)

    def p2_norm_units(g):
        held = {}

        def load(ci):
            c = 4 * g + ci
            held[ci] = load_x(slice(2 + 128 * c, 2 + 128 * c + 128), 128)

        def s1b(ci):
            xs, xr = held[ci]
            norm_s1(xs, xr, 128, G_MIXPRE, ci % 2)

        def s2(ci):
            norm_s2(128, hT, "hT", 2 + 128 * ci)
        units = []
        for k in range(7):
            def u(k=k):
                if 0 <= k - 2 < 5:
                    s2(k - 2)
                if 0 <= k - 1 < 5:
                    s1b(k - 1)
                if k < 5:
                    load(k)
            units.append(u)
        return units

    def stageA(g, inline_norm):
        if inline_norm:
            for u in p2_norm_units(g):
                u()
        pend = None
        for t in range(28):
            if t % 4 == 0:
                wv, wres = w_use((8, 512))
            wc = 128 * (t % 4)
            if t < 12:
                cur = tile_p1(t, wv, wres, wc, 2, 514, "R")
                if pend is not None:
                    tile_p2(*pend)
                pend = cur
                continue
            bank, bres = next_pA()

            def mm(e, bank=bank, wv=wv, wc=wc):
                for k in range(8):
                    ins = e.matmul(bank[:, :], lhsT=wv[:, k, wc:wc + 128], rhs=hT[:, k, 2:514], start=(k == 0), stop=(k == 7))
                return ins
            P.add("pe", mm, reads=[wres, "hT"], writes=bres)
            if pend is not None:
                tile_p2(*pend)
                pend = None
            if t < 20:
                P.add("act", lambda e, bank=bank, t=t: e.activation(out=qT[:, t - 12, :], in_=bank[:, :], func=AF.Copy, scale=0.125),
                      reads=bres, writes=["qT"])
            else:
                P.add("dve", lambda e, bank=bank, t=t: e.tensor_copy(out=kT[:, t - 20, 128:640], in_=bank[:, :]),
                      reads=bres, writes=["kT"])
                bank2, bres2 = next_pA()

                def mm2(e, bank2=bank2, wv=wv, wc=wc):
                    for k in range(8):
                        ins = e.matmul(bank2[:, 0:128], lhsT=wv[:, k, wc:wc + 128], rhs=hT[:, k, 514:642], start=(k == 0), stop=(k == 7))
                    return ins
                P.add("pe", mm2, reads=[wres, "hT"], writes=bres2)
                P.add("dve", lambda e, bank2=bank2, t=t: e.tensor_copy(out=kT[:, t - 20, 640:768], in_=bank2[:, 0:128]),
                      reads=bres2, writes=["kT"])
        for zp in range(2):
            wv, wres = w_use((8, 512))
            for ci in range(4):
                bank, bres = next_pA()

                def mm(e, bank=bank, wv=wv, ci=ci):
                    for k in range(8):
                        ins = e.matmul(bank[:, :], lhsT=hT[:, k, 2 + 128 * ci:2 + 128 * ci + 128], rhs=wv[:, k, :], start=(k == 0), stop=(k == 7))
                    return ins
                P.add("pe", mm, reads=[wres, "hT"], writes=bres)
                P.add("act", lambda e, bank=bank, ci=ci, zp=zp: e.activation(out=zs[:, ci, 512 * zp:512 * zp + 512], in_=bank[:, :], func=AF.Silu),
                      reads=bres, writes=["zs"])
        wv, wres = w_use((8, 288))
        for ci in range(5):
            bank, bres = next_pA()

            def mm(e, bank=bank, wv=wv, ci=ci):
                for k in range(8):
                    ins = e.matmul(bank[:, 0:288], lhsT=hT[:, k, 2 + 128 * ci:2 + 128 * ci + 128], rhs=wv[:, k, :], start=(k == 0), stop=(k == 7))
                return ins
            P.add("pe", mm, reads=[wres, "hT"], writes=bres)
            P.add("act", lambda e, bank=bank, ci=ci: e.activation(out=vv[:, ci + 1, :], in_=bank[:, 0:256], func=AF.Copy),
                  reads=bres, writes=["vv"])
            if ci < 4:
                P.add("dve", lambda e, bank=bank, ci=ci: e.tensor_copy(out=dtraw[:, ci, :], in_=bank[:, 256:288]),
                      reads=bres, writes=["dtraw"])
        dt_prep_group(fwd=True)

    def attn_units(g, ci, slot):
        c = 4 * g + ci
        qcols = slice(128 * ci, 128 * ci + 128)
        blocks = [j for j in range(3) if not (c == 0 and j == 0)]
        units = []
        for kg in range(4):
            for j in blocks:
                def score(kg=kg, j=j):
                    kcols = slice(128 * (ci + j), 128 * (ci + j) + 128)

                    def mm(e):
                        e.matmul(banks[2][:, :], lhsT=IDENT, rhs=abias[:, kg, j, :], start=True, stop=False, skip_group_check=True)
                        e.matmul(banks[2][:, 0:256], lhsT=kT[:, kg, kcols], rhs=qT[:, 2 * kg:2 * kg + 2, qcols], start=False, stop=False, skip_group_check=True)
                        return e.matmul(banks[2][:, 256:512], lhsT=kT[:, 4 + kg, kcols], rhs=qT[:, 2 * kg:2 * kg + 2, qcols], start=False, stop=True, skip_group_check=True)
                    P.add("pe", mm, reads=["abias", "cbf", "kT", "qT"], writes=["bk2"])
                    P.add("act", lambda e: e.activation(out=PT[:, j, :], in_=banks[2][:, :], func=AF.Exp),
                          reads=["bk2"], writes=[f"PT{j}"])
                units.append(score)

            def pvn(kg=kg):
                po = banks[5]

                def pv(e):
                    first = True
                    for j in blocks:
                        vsl = vv[:, ci + j, 64 * kg:64 * kg + 64]
                        for m in range(2):
                            rh = PT[:, j, 256 * m:256 * m + 256]
                            e.matmul(po[64 * m:64 * m + 64, 0:256], lhsT=vsl, rhs=rh, start=first, stop=False, skip_group_check=True)
                            ins = e.matmul(po[64 * m:64 * m + 64, 256:512], lhsT=ONES[:, 0:64], rhs=rh, start=False, stop=False, skip_group_check=True)
                        first = False
                    return ins
                P.add("pe", pv, reads=["vv", "cbf"] + [f"PT{j}" for j in blocks], writes=["bk5"])
                for pr in range(2):
                    P.add("act", lambda e, pr=pr: e.activation(out=rden[:, 128 * pr:128 * pr + 128], in_=po[:, 256 + 128 * pr:256 + 128 * pr + 128],
                                                               func=AF.Ln, bias=esk[:, 2 * kg + pr:2 * kg + pr + 1]),
                          reads=["bk5", "esk"], writes=["rden"])
                P.add("act", lambda e: e.activation(out=rden[:], in_=rden[:], func=AF.Exp, scale=-1.0), reads=["rden"], writes=["rden"])
                P.add("dve", lambda e: e.tensor_tensor(out=ycatT[:, slot, 8 + 2 * kg:8 + 2 * kg + 2, :],
                                                       in0=po[:, 0:256].rearrange("p (a q) -> p a q", a=2),
                                                       in1=rden[:].rearrange("p (a q) -> p a q", a=2), op=ALU.mult),
                      reads=["bk5", "rden"], writes=[f"ycatT{slot}"])
            units.append(pvn)
        return units

    def ssd_attn_chunk(g, ci, slot, with_attn=True):
        c = 4 * g + ci
        cols = slice(128 * ci, 128 * ci + 128)
        pbi = c % 2
        units = attn_units(g, ci, slot) if with_attn else []
        dma(QX, prevb[pbi][:], pb_s[c], [f"pb{c}"], [f"prevb{pbi}"], f"pbr{pbi}")
        pcb = banks[4][:, 256:512]

        def mmcb(e):
            for gi in range(2):
                ins = e.matmul(pcb[:, 128 * gi:128 * gi + 128], lhsT=xc[:, 8 + gi, cols], rhs=xc[:, 10 + gi, cols], start=True, stop=True)
            return ins
        P.add("pe", mmcb, reads=["xc8", "xc9", "xc10", "xc11"], writes=["bk4.cb"])
        pcb3 = pcb.rearrange("p (g l) -> p g l", g=2)
        P.add("dve", lambda e: e.tensor_tensor(out=CBm[:, 0, :, :], in0=pcb3, in1=UU[:, None, :].broadcast_to([128, 2, 128]), op=ALU.mult),
              reads=["bk4.cb", "cbf"], writes=["CBm0"])
        P.add("dve", lambda e: e.tensor_tensor(out=CBm[:, 1, :, :], in0=pcb3, in1=UB[:, None, :].broadcast_to([128, 2, 128]), op=ALU.mult),
              reads=["bk4.cb", "cbf"], writes=["CBm1"])
        tok_transposes(ci)
        dt_f, dt_b = smc(1, ci, 0), smc(1, ci, 1)
        for di in range(2):
            umat = UU if di == 0 else UB
            dAb = smb[:, 32 * ci + 16 * di:32 * ci + 16 * di + 16]
            ra, rres = rhs_alls[di]
            P.add("dve" if di == 0 else "pool",
                  lambda e, umat=umat, dAb=dAb, ra=ra: e.tensor_tensor(out=ra, in0=umat[:, None, :].broadcast_to([128, 16, 128]),
                                                                      in1=dAb[:, :, None].broadcast_to([128, 16, 128]), op=ALU.mult),
                  reads=["cbf", "smb"], writes=[rres])
            if di == 0:
                P.add("pool", lambda e: e.tensor_tensor(out=v3(XwS[:]), in0=v3(Xtok[:]), in1=bc_h(smc(7, ci, 0)), op=ALU.mult),
                      reads=["Xtok", "sm7"], writes=["XwS"])
                P.add("dve", lambda e: e.tensor_tensor(out=v3(XDs[:]), in0=v3(Xtok[:]), in1=v3(dskip[:]), op=ALU.mult),
                      reads=["Xtok", "dskip"], writes=["XDs"])
                P.add("dve", lambda e: e.tensor_tensor(out=v3(Xdf[:]), in0=v3(Xtok[:]), in1=bc_h(dt_f), op=ALU.mult),
                      reads=["Xtok", "sm1"], writes=["Xdf"])
        P.add("pool", lambda e: e.tensor_tensor(out=v3(Xdb[:]), in0=v3(Xtok[:]), in1=bc_h(dt_b), op=ALU.mult),
              reads=["Xtok", "sm1"], writes=["Xdb"])
        n_pre = min(5, len(units)) if INTERLEAVE else 0
        ui = 0
        while ui < n_pre:
            units[ui]()
            ui += 1
        Y = [banks[6], banks[7]]
        YR = bk(6) + bk(7)

        def y0(e):
            e.matmul(Y[0][:, :], lhsT=IDENT, rhs=XDs[:, 0:512], start=True, stop=False, skip_group_check=True)
            return e.matmul(Y[1][:, :], lhsT=IDENT, rhs=XDs[:, 512:1024], start=True, stop=False, skip_group_check=True)
        P.add("pe", y0, reads=["XDs", "cbf"], writes=YR)
        pfi = c % 2
        steps = []
        for di in range(2):
            for kind in ("D", "R"):
                if kind == "R" and di == 0 and c == 0:
                    continue
                for q in range(4):
                    steps.append((di, kind, q))
        last_dir = [None]

        def emit_D(step):
            di, kind, q = step
            rhs_all, rares = rhs_alls[di]
            if kind == "D":
                lhs = SG if di == 0 else SU
            else:
                lhs = ONES
            bi = (0, 1, 3, 4)[cnt["pd"] % 4]
            cnt["pd"] += 1
            bank, bres = banks[bi], (bk(bi) if bi != 4 else BK4)
            P.add("pe", lambda e: e.matmul(bank[:, :], lhsT=lhs, rhs=rhs_all[:, 4 * q:4 * q + 4, :], start=True, stop=True),
                  reads=[rares, "cbf"], writes=bres)
            ei = cnt["er"] % 5
            cnt["er"] += 1
            E, eres = Ering[ei], f"Er{ei}"
            P.add("act", lambda e: e.activation(out=E[:], in_=bank[:, :].rearrange("p (h l) -> p h l", h=4), func=AF.Exp),
                  reads=bres, writes=[eres])
            gi = q // 2
            if kind == "D":
                m_in = CBm[:, di, gi, :]
                mres = [f"CBm{di}"]
            else:
                m_in = xc[:, 10 + gi, cols]
                mres = [f"xc{10 + gi}"]
            P.add("dve", lambda e: e.tensor_tensor(out=E[:], in0=E[:], in1=m_in[:, None, :].broadcast_to([128, 4, 128]), op=ALU.mult),
                  reads=[eres] + mres, writes=[eres])
            return E, eres

        def emit_Y(step, E, eres):
            di, kind, q = step
            if kind == "D":
                R, rres = (Xdf, "Xdf") if di == 0 else (Xdb, "Xdb")
            else:
                R, rres = (prevf[pfi], f"prevf{pfi}") if di == 0 else (prevb[pbi], f"prevb{pbi}")

            def mm(e):
                for hh in range(4):
                    h = 4 * q + hh
                    ins = e.matmul(Y[h // 8][:, 64 * (h % 8):64 * (h % 8) + 64], lhsT=E[:, hh, :], rhs=R[:, 64 * h:64 * h + 64],
                                   start=False, stop=False, skip_group_check=True)
                return ins
            P.add("pe", mm, reads=[eres, rres], writes=YR)

        n_tail = 5 if INTERLEAVE else 0
        LA = 4
        pend = [emit_D(steps[k]) for k in range(min(LA, len(steps)))]
        for i in range(len(steps)):
            if INTERLEAVE and ui < len(units) - n_tail:
                units[ui]()
                ui += 1
            if i + LA < len(steps):
                pend.append(emit_D(steps[i + LA]))
            emit_Y(steps[i], *pend.pop(0))
        P.add("dve", lambda e: e.tensor_tensor(out=gbuf[:, 0:512], in0=banks[6][:, :], in1=zs[:, ci, 0:512], op=ALU.mult),
              reads=bk(6) + ["zs"], writes=["gbuf"])
        P.add("dve", lambda e: e.tensor_tensor(out=gbuf[:, 512:1024], in0=banks[7][:, :], in1=zs[:, ci, 512:1024], op=ALU.mult),
              reads=bk(7) + ["zs"], writes=["gbuf"])
        norm_s1(gbuf[:], "gbuf", 128, G_SSM, 3)
        Sf = state_matmuls()
        while ui < len(units):
            units[ui]()
            ui += 1
        state_update(ci, 0, Sf)
        nfi = (c + 1) % 2
        P.add("act", lambda e: e.activation(out=prevf[nfi][:], in_=stt[:], func=AF.Copy), reads=["stt"], writes=[f"prevf{nfi}"])
        norm_s2(128, ycatT[:, slot, 0:8, :], f"ycatT{slot}", 0)

    def rstd_from(sidx_list, dst_idx, nel):
        d = stat[:, dst_idx:dst_idx + 1]
        a, b = sidx_list
        P.add("dve", lambda e: e.tensor_tensor(out=d, in0=stat[:, a:a + 1], in1=stat[:, b:b + 1], op=ALU.add),
              reads=[f"stat{a}", f"stat{b}"], writes=[f"stat{dst_idx}"])
        P.add("act", lambda e: e.activation(out=d, in_=d, func=AF.Ln, scale=1.0 / nel, bias=EPSB[:]),
              reads=[f"stat{dst_idx}"], writes=[f"stat{dst_idx}"])
        P.add("act", lambda e: e.activation(out=d, in_=d, func=AF.Exp, scale=-0.5), reads=[f"stat{dst_idx}"], writes=[f"stat{dst_idx}"])

    def resid_norm_add(acc_bank, gain, s):
        for hf in range(2):
            P.add("act", lambda e, hf=hf: e.activation(out=hb[0][:, 512 * hf:512 * hf + 512], in_=acc_bank[hf][:, :], func=AF.Square,
                                                      accum_out=stat[:, 4 + hf:5 + hf]),
                  reads=ACCB, writes=["hb0", f"stat{4 + hf}"])
            P.add("dve", lambda e, hf=hf: e.tensor_tensor(out=gbuf[:, 512 * hf:512 * hf + 512], in0=acc_bank[hf][:, :],
                                                         in1=gain[:, 512 * hf:512 * hf + 512], op=ALU.mult),
                  reads=ACCB + ["gains"], writes=["gbuf"])
        rstd_from([4, 5], 6, D)
        P.add("dve", lambda e: e.scalar_tensor_tensor(out=x1[:, s, :], in0=gbuf[:], scalar=stat[:, 6:7], in1=x1[:, s, :],
                                                      op0=ALU.mult, op1=ALU.add),
              reads=["gbuf", "stat6", "x1"], writes=["x1"])

    def outproj_pair(g, pair):
        c0 = 4 * g + 2 * pair
        dma(QX, x1[:], x_d[2 + 128 * c0:2 + 128 * c0 + 256, :].rearrange("(s p) d -> p s d", p=128), [], ["x1"], "x1")
        for i in (2, 3, 0, 1):
            wv, wres = w_use((4, 1024))

            def mm(e, wv=wv, i=i):
                for kk in range(4):
                    kt = 4 * i + kk
                    for s in range(2):
                        for hf in range(2):
                            ins = e.matmul(banks[4 + 2 * s + hf][:, :], lhsT=ycatT[:, s, kt, :], rhs=wv[:, kk, 512 * hf:512 * hf + 512],
                                           start=(kt == 8), stop=(kt == 7))
                return ins
            P.add("pe", mm, reads=[wres, "ycatT0", "ycatT1"], writes=ACCB)
        for s in range(2):
            resid_norm_add([banks[4 + 2 * s], banks[5 + 2 * s]], G_MIXPOST, s)

    def ffn_pair(g, pair, hoist_next):
        c0 = 4 * g + 2 * pair
        hoist = p2_norm_units(g + 1) if hoist_next else []
        for s in range(2):
            norm_T(x1[:, s, :], "x1", 128, G_MLPPRE, h2T, "h2T", 128 * s, 7)
        acc = [[banks[4 + 2 * s + hf] for hf in range(2)] for s in range(2)]
        upq = []
        state = {"wu": None, "wd": None}

        def emit_up(j):
            u, jj = divmod(j, 4)
            if jj == 0:
                state["wu"] = w_use((8, 512))
                state["wd"] = w_use((4, 1024))
            wu, ru = state["wu"]
            wd, rd = state["wd"]
            pi = cnt["pu"] % 3
            cnt["pu"] += 1
            ub = banks[pi][:, 0:256]
            ures = [f"bk{pi}"]

            def mmu(e):
                for k in range(8):
                    ins = e.matmul(ub, lhsT=wu[:, k, 128 * jj:128 * jj + 128], rhs=h2T[:, k, :], start=(k == 0), stop=(k == 7))
                return ins
            P.add("pe", mmu, reads=[ru, "h2T"], writes=ures)
            fi = cnt["fr"] % 4
            cnt["fr"] += 1
            fr, fres = fring[fi], f"fr{fi}"
            P.add("act", lambda e: e.activation(out=fr[:], in_=ub, func=AF.Relu), reads=ures, writes=[fres])
            P.add("dve", lambda e: e.tensor_tensor(out=fr[:], in0=fr[:], in1=fr[:], op=ALU.mult), reads=[fres], writes=[fres])
            upq.append((fr, fres, wd, rd, jj, j))

        def emit_down():
            fr, fres, wd, rd, jj, j = upq.pop(0)

            def mmd(e):
                for s in range(2):
                    for hf in range(2):
                        ins = e.matmul(acc[s][hf][:, :], lhsT=fr[:, 128 * s:128 * s + 128], rhs=wd[:, jj, 512 * hf:512 * hf + 512],
                                       start=(j == 0), stop=(j == 31))
                return ins
            P.add("pe", mmd, reads=[rd, fres], writes=ACCB)

        for j in range(32):
            emit_up(j)
            if j % 4 >= 2:
                emit_down()
            if j % 4 == 3:
                emit_down()
                emit_down()
                if hoist:
                    hoist.pop(0)()
        assert not upq
        while hoist:
            hoist.pop(0)()
        for s in range(2):
            c = c0 + s
            resid_norm_add(acc[s], G_MLPPOST, s)
            dma(QX, out_d[128 * c:128 * c + 128, :], x1[:, s, :], ["x1"], [f"out{c}"], "outd")

    def carry_kv():
        P.add("pool", lambda e: e.tensor_copy(out=kT[:, :, 0:128], in_=kT[:, :, 512:640]), reads=["kT"], writes=["kT"])
        P.add("pool", lambda e: e.tensor_copy(out=vv[:, 0, :], in_=vv[:, 4, :]), reads=["vv"], writes=["vv"])

    def chunks(g, pair, wa=True):
        for s_ in range(2):
            ssd_attn_chunk(g, 2 * pair + s_, s_, with_attn=wa)

    def phase2_group(g):
        last = (g + 1 >= n_groups2)
        if not PIPE2:
            stageA(g, inline_norm=(g == 0 or stop is not None))
            if stop == "A":
                return
            for pair in range(2):
                chunks(g, pair, wa=(stop != "ssd"))
                if stop in ("ssd", "attn"):
                    continue
                outproj_pair(g, pair)
                if stop == "outproj":
                    continue
                ffn_pair(g, pair, hoist_next=(pair == 1 and not last and stop is None))
            carry_kv()
            return
        if g == 0:
            stageA(0, inline_norm=True)
        chunks(g, 0)
        outproj_pair(g, 0)
        chunks(g, 1)
        carry_kv()
        ffn_pair(g, 0, hoist_next=not last)
        outproj_pair(g, 1)
        if not last:
            stageA(g + 1, inline_norm=False)
        ffn_pair(g, 1, hoist_next=False)

    setup()
    phase1_all()
    emit_casts(1000)
    if n_groups2 > 0:
        P.add("pool", lambda e: e.memset(stt[:], 0.0), reads=["stt"], writes=["stt"])
        P.add("pool", lambda e: e.memset(ucarry[:], 0.0), writes=[f"uc{t}" for t in range(12)])
    for g in range(n_groups2):
        phase2_group(g)
    outs = ([f"out{c}" for c in range(4 * n_groups2)] if stop is None else []) + ["dbg_" + k for k in dbg_outs]
    P.add("sp", None, reads=outs + [f"pb{c}" for c in range(min(32, 4 * n_groups1))])
    P.finalize(st)
    P.emit()
    print("IR ops:", len(P.ops))
    return nc, st, dbg_outs


def _consts():
    a = np.arange(128)[:, None]
    b = np.arange(128)[None, :]
    mats = [a == b, a <= b, a >= b, a < b, a > b, np.ones((128, 128), bool)]
    cbf = np.stack([m.astype(np.float32) for m in mats], 1).astype(NPBF)
    cf32 = np.stack([(a <= b), (a < b), np.ones((128, 128), bool)], 1).astype(np.float32)
    slopes = 2.0 ** (-8.0 * np.arange(1, 17, dtype=np.float64) / 16)
    ab = np.zeros((128, 4, 3, 4, 128), np.float32)
    k = np.arange(128)[:, None]
    q = np.arange(128)[None, :]
    for g in range(4):
        for j in range(3):
            dist = (k + 128 * (j - 1)) - q
            valid = np.abs(dist) <= 128
            for pos, r in enumerate((0, 2, 1, 3)):
                h = 4 * g + r
                ab[:, g, j, pos, :] = np.where(valid, -slopes[h] * np.abs(dist), -30000.0)
    return cbf, cf32, ab.reshape(128, 4, 3, 512).astype(NPBF)


def _prep_core(inp, b, half):
    f = lambda a: np.ascontiguousarray(np.asarray(a, dtype=np.float32))
    x = f(inp["x"])[b]
    w_in = f(inp["w_in"])[0]
    conv_w = f(inp["conv_w"])[0]
    dt_bias = f(inp["dt_bias"])[0]
    a_log = f(inp["a_log"])[0]
    if half == 1:
        x = x[::-1]
        conv_w = conv_w[::-1]
        dt_bias = dt_bias[::-1]
        a_log = a_log[::-1]
    xp = np.zeros((NLOC + 2, D), np.float32)
    xp[2:] = x
    z = w_in[:, 0:1024]
    xs = w_in[:, 1024:2048]
    Bc = w_in[:, 2048:2304]
    Cc = w_in[:, 2304:2560]
    dtf = w_in[:, 2560:2576]
    dtb_ = w_in[:, 2576:2592]
    q = w_in[:, 2592:3616]
    k = w_in[:, 3616:3872]
    v = w_in[:, 3872:4128]
    zz = np.zeros((D, 64), np.float32)
    kd = np.concatenate([np.concatenate([k[:, 64 * g:64 * g + 64], zz], 1) for g in range(4)]
                        + [np.concatenate([zz, k[:, 64 * g:64 * g + 64]], 1) for g in range(4)], 1)
    dts = [dtf, dtb_] if half == 0 else [dtb_, dtf]
    win = np.concatenate([xs, Bc, Cc, q, kd, z, v] + dts, 1)
    assert win.shape == (D, WIN_COLS)
    rep = lambda vec: np.ascontiguousarray(np.broadcast_to(vec[None], (128,) + vec.shape))
    gains = np.stack([f(inp[n])[0] for n in ("norm_mix_pre", "ssm_norm", "norm_mix_post", "norm_mlp_pre", "norm_mlp_post")], 0)
    sink = f(inp["attn_sink"])[0]
    sink_t = np.zeros((128, 8), np.float32)
    for pr in range(8):
        sink_t[0:64, pr] = sink[2 * pr]
        sink_t[64:128, pr] = sink[2 * pr + 1]
    cw = conv_w.T.reshape(12, 128, 5).transpose(1, 0, 2)
    cb = f(inp["conv_b"])[0].reshape(12, 128).T
    return {
        "x": xp, "w_in": np.ascontiguousarray(win), "w_out": f(inp["w_out"])[0], "w_up": f(inp["w_up"])[0],
        "w_down": f(inp["w_down"])[0], "gains": rep(gains), "dskip": rep(np.repeat(f(inp["d_skip"])[0], 64)),
        "convw": np.ascontiguousarray(cw), "convb": np.ascontiguousarray(cb),
        "dtb": rep(dt_bias.reshape(32)), "alog": rep(a_log.reshape(32)), "sink": sink_t,
    }


_CACHE = {}


def kernel(**inputs):
    if "nc" not in _CACHE:
        _CACHE["nc"] = build()
    nc, st, _ = _CACHE["nc"]
    cbf, cf32, abias = _consts()
    in_maps = []
    for core in range(8):
        b, half = core // 2, core % 2
        m = _prep_core(inputs, b, half)
        m.update({"cbf": cbf, "cf32": cf32, "abias": abias})
        in_maps.append(m)
    res = run_bass_kernel_spmd(nc, in_maps, core_ids=list(range(8)))
    out = np.zeros((4, 8192, D), np.float32)
    for core in range(8):
        b, half = core // 2, core % 2
        o = np.asarray(res.results[core]["out"], dtype=np.float32)
        if half == 0:
            out[b, 0:NOWN] = o
        else:
            out[b, NOWN:] = o[::-1]
    return out
```

```python
import os
from contextlib import ExitStack
import numpy as np
import ml_dtypes
import concourse.bass as bass
import concourse.mybir as mybir
from concourse.bass_utils import run_bass_kernel_spmd

F32 = mybir.dt.float32
BF16 = mybir.dt.bfloat16
AF = mybir.ActivationFunctionType
ALU = mybir.AluOpType
NPBF = ml_dtypes.bfloat16

D = 1024
NLOC = 8192
NOWN = 4096
WIN_COLS = 4896
EPS = 1e-6
DEBUG = {}

ENGS = ("pe", "act", "dve", "pool", "sp")


class Op:
    __slots__ = ("eng", "fn", "reads", "writes", "dkey", "waits", "sig", "dcount")

    def __init__(self, eng, fn, reads, writes, dkey):
        self.eng, self.fn, self.reads, self.writes, self.dkey = eng, fn, reads, writes, dkey
        self.waits = []
        self.sig = None
        self.dcount = None


class Prog:
    def __init__(self, nc):
        self.nc = nc
        self.ops = []

    def add(self, eng, fn, reads=(), writes=(), dkey=None):
        self.ops.append(Op(eng, fn, tuple(reads), tuple(writes), dkey))

    def finalize(self, stack):
        nc, ops = self.nc, self.ops
        last_w, readers, deps_of = {}, {}, []
        canon = lambda r: r[:3] if r.startswith("bk") else r
        for i, op in enumerate(ops):
            deps = set()
            rds = {canon(r) for r in op.reads}
            wrs = {canon(r) for r in op.writes}
            for r in rds:
                w = last_w.get(r)
                if w is not None:
                    deps.add(w)
                if r.startswith("bk"):
                    deps.update(j for j in readers.get(r, ()) if ops[j].eng != op.eng)
            for r in wrs:
                w = last_w.get(r)
                if w is not None:
                    deps.add(w)
                deps.update(readers.get(r, ()))
            deps.discard(i)
            for r in rds:
                readers.setdefault(r, []).append(i)
            for r in wrs:
                last_w[r] = i
                readers[r] = []
            deps = {d for d in deps if not (ops[d].eng == "pe" and op.eng == "pe"
                                            and ops[d].dkey is None and op.dkey is None)}
            deps_of.append(deps)
        needed = set()
        for deps in deps_of:
            for d in deps:
                if ops[d].dkey is None:
                    needed.add(d)
        cnt, dcnt = {}, {}
        for i, op in enumerate(ops):
            if op.dkey is not None:
                dcnt[op.dkey] = dcnt.get(op.dkey, 0) + 1
                op.dcount = dcnt[op.dkey]
            elif i in needed:
                cnt[op.eng] = cnt.get(op.eng, 0) + 1
                op.sig = cnt[op.eng]
        self.sems = {}
        for e in ENGS:
            self.sems[("c", e)] = stack.enter_context(nc.semaphore("s_" + e))
        for k in dcnt:
            self.sems[("d", k)] = stack.enter_context(nc.semaphore("d_" + str(k)))
        waited = {}
        for i, op in enumerate(ops):
            req = {}
            for d in deps_of[i]:
                p = ops[d]
                if p.dkey is not None:
                    key, val = ("d", p.dkey), 16 * p.dcount
                else:
                    key, val = ("c", p.eng), p.sig
                if val > req.get(key, 0):
                    req[key] = val
            for key, val in req.items():
                if waited.get((op.eng, key), 0) < val:
                    waited[(op.eng, key)] = val
                    op.waits.append((key, val))

    def emit(self):
        nc, ops, sems = self.nc, self.ops, self.sems

        def run(engname, e):
            for op in ops:
                if op.eng != engname:
                    continue
                for key, val in op.waits:
                    e.wait_ge(sems[key], val)
                if op.fn is None:
                    continue
                ins = op.fn(e)
                if op.dkey is not None:
                    ins.then_inc(sems[("d", op.dkey)], 16)
                elif op.sig is not None:
                    ins.then_inc(sems[("c", op.eng)], 1)

        with nc.Block() as block:
            @block.sync
            def _(e):
                run("sp", e)

            @block.scalar
            def _(e):
                run("act", e)

            @block.vector
            def _(e):
                run("dve", e)

            @block.gpsimd
            def _(e):
                run("pool", e)

            @block.tensor
            def _(e):
                run("pe", e)


def build(n_groups2=8, n_groups1=16, debug=None, stop=None):
    INTERLEAVE = bool(int(os.environ.get('MK_INTERLEAVE', '1')))
    QX = os.environ.get('MK_QX', 'sp')
    PIPE2 = (stop is None) and bool(int(os.environ.get('MK_PIPE2', '1')))
    nc = bass.Bass("TRN2", target_bir_lowering=False)
    st = ExitStack()
    P = Prog(nc)
    din = lambda name, shape, dt=F32: nc.dram_tensor(name, list(shape), dt, kind="ExternalInput").ap()
    x_d = din("x", [NLOC + 2, D])
    win_d = din("w_in", [D, WIN_COLS])
    wout_d = din("w_out", [2048, D])
    wup_d = din("w_up", [D, 4096])
    wdn_d = din("w_down", [4096, D])
    gains_d = din("gains", [128, 5, D])
    dskip_d = din("dskip", [128, D])
    convw_d = din("convw", [128, 12, 5])
    convb_d = din("convb", [128, 12])
    dtb_d = din("dtb", [128, 32])
    alog_d = din("alog", [128, 32])
    sink_d = din("sink", [128, 8])
    cbf_d = din("cbf", [128, 6, 128], BF16)
    cf32_d = din("cf32", [128, 3, 128])
    abias_d = din("abias", [128, 4, 3, 512], BF16)
    out_d = nc.dram_tensor("out", [NOWN, D], F32, kind="ExternalOutput").ap()
    win_s = nc.dram_tensor("win_s", [D, WIN_COLS], BF16).ap()
    wout_s = nc.dram_tensor("wout_s", [2048, D], BF16).ap()
    wup_s = nc.dram_tensor("wup_s", [D, 4096], BF16).ap()
    wdn_s = nc.dram_tensor("wdn_s", [4096, D], BF16).ap()
    pb_s = nc.dram_tensor("pb_s", [32, 128, D], BF16).ap()
    dbg_outs = {}

    def sb(name, shape, dt):
        return st.enter_context(nc.sbuf_tensor("sb_" + name, list(shape), dt))

    cbf = sb("cbf", [128, 6, 128], BF16)
    cf32 = sb("cf32", [128, 3, 128], F32)
    abias = sb("abias", [128, 4, 3, 512], BF16)
    gains = sb("gains", [128, 5, D], BF16)
    dskip = sb("dskipb", [128, D], BF16)
    cdiag = sb("cdiag", [128, 12, 5, 128], BF16)
    convw = sb("convw", [128, 12, 5], F32)
    convb = sb("convb", [128, 12], F32)
    dtb = sb("dtb", [128, 32], F32)
    negA = sb("negA", [128, 32], F32)
    esk = sb("esk", [128, 8], F32)
    EPSB = sb("epsb", [128, 1], F32)
    MHALF = sb("mhalf", [128, 1], F32)
    IDENT, UU, UB, SU, SG, ONES = (cbf[:, i, :] for i in range(6))
    U_F, SU_F, ONES_F = (cf32[:, i, :] for i in range(3))
    G_MIXPRE, G_SSM, G_MIXPOST, G_MLPPRE, G_MLPPOST = (gains[:, i, :] for i in range(5))
    xin = [sb(f"xin{i}", [128, D], F32) for i in range(2)]
    hT = sb("hT", [128, 8, 642], BF16)
    hb = [sb(f"hb{i}", [128, D], BF16) for i in range(1)]
    stat = sb("stat", [128, 8], F32)
    u_bf = [sb(f"u_bf{i}", [128, 516], BF16) for i in range(2)]
    ucarry = sb("ucarry", [128, 12, 2], BF16)
    xc = sb("xc", [128, 12, 512], BF16)
    qT = sb("qT", [128, 8, 512], BF16)
    kT = sb("kT", [128, 8, 768], BF16)
    vv = sb("vv", [128, 6, 256], BF16)
    zs = sb("zs", [128, 4, D], BF16)
    dtraw = sb("dtraw", [128, 4, 32], F32)
    Xtok = sb("Xtok", [128, D], BF16)
    Xdf = sb("Xdf", [128, D], BF16)
    Xdb = sb("Xdb", [128, D], BF16)
    XwS = sb("XwS", [128, D], BF16)
    XDs = sb("XDs", [128, D], BF16)
    Btok = sb("Btok", [128, 256], BF16)
    rhs_all0 = sb("rhs_all", [128, 16, 128], BF16)
    Ering = [sb(f"Er{i}", [128, 4, 128], BF16) for i in range(5)]
    CBm = sb("CBm", [128, 2, 2, 128], BF16)
    smb = sb("smb", [128, 128], BF16)
    sm = sb("sm", [128, 9, 128], F32)
    stt = sb("stt", [128, D], F32)
    prevf = [sb(f"prevf{i}", [128, D], BF16) for i in range(2)]
    prevb = [sb(f"prevb{i}", [128, D], BF16) for i in range(2)]
    PT = sb("PT", [128, 3, 512], BF16)
    rden = sb("rden", [128, 256], F32)
    ycatT = sb("ycatT", [128, 2, 16, 128], BF16)
    gbuf = sb("gbuf", [128, D], F32)
    x1 = sb("x1", [128, 2, D], F32)
    h2T = sb("h2T", [128, 8, 256], BF16)
    rhs_alls = [(rhs_all0[:], "rhs_all"), (h2T[:].rearrange("p a (b c) -> p (a b) c", b=2), "h2T")]
    fring = [sb(f"fr{i}", [128, 256], BF16) for i in range(4)]
    NRING = 4
    wring = [sb(f"wr{i}", [128, 4096], BF16) for i in range(NRING)]
    banks = [st.enter_context(nc.psum_tensor(f"bk{i}", [128, 512], F32)) for i in range(8)]
    print("sbuf bytes remaining:", nc.sbuf_bytes_remaining)

    def bk(i):
        return [f"bk{i}a", f"bk{i}b"]
    BK4 = ["bk4.uh", "bk4.bt", "bk4.cb"]
    ACCB = BK4 + bk(5) + bk(6) + bk(7)

    cnt = {"pd": 0, "pA": 0, "xin": 0, "u": 0, "er": 0, "fr": 0, "pu": 0, "hb": 0, "ev": 0}

    def dma(eng, out, in_, reads, writes, key):
        P.add(eng, lambda e: e.dma_start(out=out, in_=in_), reads=reads, writes=writes, dkey=key)

    wseq = []
    wstate = {"issued": 0, "used": 0}

    def w_issue_upto(n):
        while wstate["issued"] < min(n, len(wseq)):
            i = wstate["issued"]
            src, shape, sres = wseq[i]
            slot = i % NRING
            nel = shape[0] * shape[1]
            dst = wring[slot][:, 0:nel].rearrange("p (a b) -> p a b", a=shape[0])
            dma("sp", dst, src, [sres], [f"wr{slot}"], f"wr{slot}")
            wstate["issued"] += 1

    def w_use(shape):
        i = wstate["used"]
        wstate["used"] += 1
        w_issue_upto(i + NRING - 1)
        slot = i % NRING
        assert tuple(wseq[i][1]) == tuple(shape), (i, wseq[i][1], shape)
        nel = shape[0] * shape[1]
        v = wring[slot][:, 0:nel].rearrange("p (a b) -> p a b", a=shape[0])
        return v, f"wr{slot}"

    def win_piece(c0, ncols):
        res = "win_p1" if (c0 + ncols <= 1280 or c0 >= 4608) else "win_rest"
        return (win_s[:, c0:c0 + ncols].rearrange("(kt p) c -> p kt c", p=128), (8, ncols), res)

    def seq_phase1_group():
        return [win_piece(0, 512), win_piece(512, 512), win_piece(1024, 256), win_piece(4608, 288)]

    def seq_stageA():
        s = [win_piece(0, 512), win_piece(512, 512)]
        s.append((win_s[:, 1024:1536].rearrange("(kt p) c -> p kt c", p=128), (8, 512), "win_rest"))
        s += [win_piece(512 * i, 512) for i in range(3, 7)]
        s += [win_piece(3584, 512), win_piece(4096, 512), win_piece(4608, 288)]
        return s

    def seq_outproj():
        return [(wout_s[512 * i:512 * i + 512, :].rearrange("(kt p) c -> p kt c", p=128), (4, 1024), "wout_s") for i in (2, 3, 0, 1)]

    def seq_ffn():
        s = []
        for u in range(8):
            s.append((wup_s[:, 512 * u:512 * u + 512].rearrange("(kt p) c -> p kt c", p=128), (8, 512), "wup_s"))
            s.append((wdn_s[512 * u:512 * u + 512, :].rearrange("(j p) c -> p j c", p=128), (4, 1024), "wdn_s"))
        return s

    def seq_phase2_all():
        s = []
        if n_groups2 == 0:
            return s
        s += seq_stageA()
        for g in range(n_groups2):
            if PIPE2:
                s += seq_outproj() + seq_ffn() + seq_outproj()
                if g + 1 < n_groups2:
                    s += seq_stageA()
                s += seq_ffn()
            else:
                s += seq_outproj() + seq_ffn() + seq_outproj() + seq_ffn()
                if g + 1 < n_groups2:
                    s += seq_stageA()
        return s

    for _ in range(n_groups1):
        wseq.extend(seq_phase1_group())
    wseq.extend(seq_phase2_all())

    def next_pA():
        i = cnt["pA"] % 2
        cnt["pA"] += 1
        return banks[i], bk(i)

    def evac_copy(out, in_, reads, writes):
        cnt["ev"] += 1
        if cnt["ev"] % 2:
            P.add("dve", lambda e: e.tensor_copy(out=out, in_=in_), reads=reads, writes=writes)
        else:
            P.add("act", lambda e: e.activation(out=out, in_=in_, func=AF.Copy), reads=reads, writes=writes)

    def norm_s1(xsrc, xres, nrows, gain, sidx, use_pool=False):
        s0 = stat[0:nrows, sidx:sidx + 1]
        hbb, hres = hb[0], "hb0"
        P.add("act", lambda e: e.activation(out=hbb[0:nrows, :], in_=xsrc, func=AF.Square, accum_out=s0),
              reads=[xres], writes=[hres, f"stat{sidx}"])
        if use_pool:
            P.add("pool", lambda e: e.tensor_scalar(out=s0, in0=s0, scalar1=1.0 / D, scalar2=EPS, op0=ALU.mult, op1=ALU.add),
                  reads=[f"stat{sidx}"], writes=[f"stat{sidx}"])
            P.add("pool", lambda e: e.tensor_tensor(out=s0, in0=s0, in1=MHALF[0:nrows, :], op=ALU.pow),
                  reads=[f"stat{sidx}", "mhalf"], writes=[f"stat{sidx}"])
        else:
            P.add("act", lambda e: e.activation(out=s0, in_=s0, func=AF.Ln, scale=1.0 / D, bias=EPSB[0:nrows, :]),
                  reads=[f"stat{sidx}"], writes=[f"stat{sidx}"])
            P.add("act", lambda e: e.activation(out=s0, in_=s0, func=AF.Exp, scale=-0.5),
                  reads=[f"stat{sidx}"], writes=[f"stat{sidx}"])
        P.add("dve", lambda e: e.scalar_tensor_tensor(out=hbb[0:nrows, :], in0=xsrc, scalar=s0, in1=gain[0:nrows, :],
                                                      op0=ALU.mult, op1=ALU.mult),
              reads=[xres, f"stat{sidx}"], writes=[hres])

    def norm_s2(nrows, dstT, dres, off):
        hbb, hres = hb[0], "hb0"
        pT = banks[3][:].bitcast(BF16)

        def tr(e):
            for k in range(8):
                ins = e.transpose(out=pT[:, 128 * k:128 * k + nrows], in_=hbb[0:nrows, 128 * k:128 * k + 128],
                                  identity=IDENT[0:nrows, 0:nrows])
            return ins
        P.add("pe", tr, reads=[hres], writes=["bk3"])
        src = pT.rearrange("p (k n) -> p k n", k=8)[:, :, 0:nrows]
        evac_copy(dstT[:, :, off:off + nrows], src, ["bk3"], [dres])

    def norm_T(xsrc, xres, nrows, gain, dstT, dres, off, sidx):
        norm_s1(xsrc, xres, nrows, gain, sidx)
        norm_s2(nrows, dstT, dres, off)

    def staged(units2):
        out = []
        for i in range(len(units2) + 1):
            def u(i=i):
                if i > 0:
                    units2[i - 1][1]()
                if i < len(units2):
                    units2[i][0]()
            out.append(u)
        return out

    def setup():
        P.add("pool", lambda e: e.memset(EPSB[:], EPS), writes=["epsb"])
        P.add("pool", lambda e: e.memset(MHALF[:], -0.5), writes=["mhalf"])
        for i in range(8):
            r = slice(128 * i, 128 * i + 128)
            dma("pool", win_s[r, 0:1280], win_d[r, 0:1280], [], ["win_p1"], "cast_p1")
            dma("pool", win_s[r, 4608:4896], win_d[r, 4608:4896], [], ["win_p1"], "cast_p1")
        dma("sp", cbf[:], cbf_d, [], ["cbf"], "c_cbf")
        dma("sp", cf32[:], cf32_d, [], ["cf32"], "c_cf32")
        dma("sp", convw[:], convw_d, [], ["convw"], "c_convw")
        dma("sp", convb[:], convb_d, [], ["convb"], "c_convb")
        dma("sp", dtb[:], dtb_d, [], ["dtb"], "c_dtb")
        dma("sp", negA[:], alog_d, [], ["negA"], "c_negA")
        dma("sp", esk[:], sink_d, [], ["esk"], "c_esk")
        dma("sp", abias[:], abias_d, [], ["abias"], "c_abias")
        for i in range(5):
            dma("sp", gbuf[:], gains_d[:, i, :], [], ["gbuf"], "gf")
            P.add("dve", lambda e, i=i: e.tensor_copy(out=gains[:, i, :], in_=gbuf[:]), reads=["gbuf"], writes=["gains"])
        dma("sp", gbuf[:], dskip_d, [], ["gbuf"], "gf")
        P.add("dve", lambda e: e.tensor_copy(out=dskip[:], in_=gbuf[:]), reads=["gbuf"], writes=["dskip"])
        P.add("act", lambda e: e.activation(out=negA[:], in_=negA[:], func=AF.Exp), reads=["negA"], writes=["negA"])
        P.add("dve", lambda e: e.tensor_scalar_mul(out=negA[:], in0=negA[:], scalar1=-1.0), reads=["negA"], writes=["negA"])
        P.add("act", lambda e: e.activation(out=esk[:], in_=esk[:], func=AF.Exp), reads=["esk"], writes=["esk"])
        for t in range(12):
            for k in range(5):
                P.add("pool", lambda e, t=t, k=k: e.tensor_scalar_mul(out=cdiag[:, t, k, :], in0=IDENT,
                                                                      scalar1=convw[:, t, k:k + 1]),
                      reads=["cbf", "convw"], writes=["cdiag"])
        P.add("pool", lambda e: e.memset(ucarry[:], 0.0), writes=[f"uc{t}" for t in range(12)])
        P.add("pool", lambda e: e.memset(stt[:], 0.0), writes=["stt"])
        for i in range(8):
            r = slice(128 * i, 128 * i + 128)
            cast_jobs.append((win_s[r, 1280:2944], win_d[r, 1280:2944], "win_rest", "cast_rest"))
            cast_jobs.append((win_s[r, 2944:4608], win_d[r, 2944:4608], "win_rest", "cast_rest"))
        for i in range(16):
            cast_jobs.append((wout_s[128 * i:128 * i + 128, :], wout_d[128 * i:128 * i + 128, :], "wout_s", "cast_out"))
        for i in range(8):
            cast_jobs.append((wup_s[128 * i:128 * i + 128, 0:2048], wup_d[128 * i:128 * i + 128, 0:2048], "wup_s", "cast_up"))
            cast_jobs.append((wup_s[128 * i:128 * i + 128, 2048:4096], wup_d[128 * i:128 * i + 128, 2048:4096], "wup_s", "cast_up"))
        for i in range(32):
            cast_jobs.append((wdn_s[128 * i:128 * i + 128, :], wdn_d[128 * i:128 * i + 128, :], "wdn_s", "cast_dn"))

    cast_jobs = []

    def emit_casts(n):
        for _ in range(min(n, len(cast_jobs))):
            dst, src, res, key = cast_jobs.pop(0)
            dma("pool", dst, src, [], [res], key)

    def load_x(rows, nrows):
        i = cnt["xin"] % 2
        cnt["xin"] += 1
        dma(QX, xin[i][0:nrows, :], x_d[rows, :], [], [f"xin{i}"], f"xin{i}")
        return xin[i][0:nrows, :], f"xin{i}"

    def tile_p1(t, wv, wres, wcol, main_lo, halo_lo, halo_side, hT=hT, hres="hT"):
        bank, bres = next_pA()

        def mm(e):
            for k in range(8):
                ins = e.matmul(bank[:, :], lhsT=wv[:, k, wcol:wcol + 128], rhs=hT[:, k, main_lo:main_lo + 512],
                               start=(k == 0), stop=(k == 7))
            return ins
        P.add("pe", mm, reads=[wres, hres], writes=bres)
        hp = banks[4][:, 224 + 2 * t:226 + 2 * t]

        def mmh(e):
            for k in range(8):
                ins = e.matmul(hp, lhsT=wv[:, k, wcol:wcol + 128], rhs=hT[:, k, halo_lo:halo_lo + 2],
                               start=(k == 0), stop=(k == 7))
            return ins
        P.add("pe", mmh, reads=[wres, hres], writes=["bk4.uh"])
        ui = cnt["u"] % 2
        cnt["u"] += 1
        ub, ures = u_bf[ui], f"u_bf{ui}"
        P.add("act", lambda e: e.activation(out=ub[:, 2:514], in_=bank[:, :], func=AF.Copy), reads=bres, writes=[ures + "m"])
        if halo_side == "R":
            P.add("dve", lambda e: e.tensor_copy(out=ub[:, 514:516], in_=hp), reads=["bk4.uh"], writes=[ures + "r"])
            P.add("dve", lambda e: e.tensor_copy(out=ub[:, 0:2], in_=ucarry[:, t, :]), reads=[f"uc{t}"], writes=[ures + "l"])
            P.add("dve", lambda e: e.tensor_copy(out=ucarry[:, t, :], in_=ub[:, 512:514]), reads=[ures + "m"], writes=[f"uc{t}"])
        else:
            P.add("dve", lambda e: e.tensor_copy(out=ub[:, 0:2], in_=hp), reads=["bk4.uh"], writes=[ures + "l"])
            P.add("dve", lambda e: e.tensor_copy(out=ub[:, 514:516], in_=ucarry[:, t, :]), reads=[f"uc{t}"], writes=[ures + "r"])
            P.add("dve", lambda e: e.tensor_copy(out=ucarry[:, t, :], in_=ub[:, 2:4]), reads=[ures + "m"], writes=[f"uc{t}"])
        return t, ub, [ures + "m", ures + "l", ures + "r"]

    def xct(xb, t):
        if xb == 0:
            return xc[:, t, :], f"xc{t}"
        if t < 8:
            return qT[:, t, :], f"xq{t}"
        return ycatT[:, 0, 4 * (t - 8):4 * (t - 8) + 4, :].rearrange("p a b -> p (a b)"), f"xq{t}"

    def tile_p2(t, ub, ures, xb=0):
        def cv(e):
            for k in range(5):
                ins = e.matmul(banks[2][:, :], lhsT=cdiag[:, t, k, :], rhs=ub[:, k:k + 512], start=(k == 0), stop=(k == 4))
            return ins
        P.add("pe", cv, reads=ures + ["cdiag"], writes=bk(2))
        xo, xres_ = xct(xb, t)
        P.add("act", lambda e: e.activation(out=xo, in_=banks[2][:, :], func=AF.Silu, bias=convb[:, t:t + 1]),
              reads=bk(2) + ["convb"], writes=[xres_])

    def dt_prep_group(fwd):
        dt_prep_a()
        dt_prep_b(fwd)

    def dt_prep_a():
        S = lambda i: sm[:, i, :]
        S3 = lambda i: sm[:, i, :].rearrange("p (c k) -> p c k", c=4)
        bc4 = lambda ap: ap[:, None, :].broadcast_to([128, 4, 32])
        P.add("dve", lambda e: e.tensor_tensor(out=S3(0), in0=dtraw[:], in1=bc4(dtb[:]), op=ALU.add),
              reads=["dtraw", "dtb"], writes=["sm0"])
        P.add("act", lambda e: e.activation(out=S(0), in_=S(0), func=AF.Exp), reads=["sm0"], writes=["sm0"])
        P.add("act", lambda e: e.activation(out=S(1), in_=S(0), func=AF.Ln, bias=1.0), reads=["sm0"], writes=["sm1"])
        P.add("dve", lambda e: e.tensor_tensor(out=S3(2), in0=S3(1), in1=bc4(negA[:]), op=ALU.mult), reads=["sm1", "negA"], writes=["sm2"])
        P.add("act", lambda e: e.activation(out=S(3), in_=S(1), func=AF.Ln), reads=["sm1"], writes=["sm3"])
        P.add("dve", lambda e: e.tensor_copy(out=smb[:], in_=S(2)), reads=["sm2"], writes=["smb"])

    def dt_prep_b(fwd):
        S = lambda i: sm[:, i, :]
        S3 = lambda i: sm[:, i, :].rearrange("p (c k) -> p c k", c=4)
        bank, bres = next_pA()

        def mm(e):
            e.matmul(bank[:, 0:128], lhsT=U_F, rhs=S(2), start=True, stop=True)
            e.matmul(bank[:, 128:256], lhsT=SU_F, rhs=S(2), start=True, stop=True)
            return e.matmul(bank[:, 256:384], lhsT=ONES_F, rhs=S(2), start=True, stop=True)
        P.add("pe", mm, reads=["sm2", "cf32"], writes=bres)
        P.add("dve", lambda e: e.tensor_copy(out=sm[:, 4:7, :], in_=bank[:, 0:384].rearrange("p (a b) -> p a b", a=3)),
              reads=bres, writes=["sm456"])
        if fwd:
            sl = lambda i: S3(i)[:, :, 0:16]
            P.add("dve", lambda e: e.tensor_tensor(out=sl(7), in0=sl(6), in1=sl(4), op=ALU.subtract), reads=["sm456"], writes=["sm7"])
            P.add("dve", lambda e: e.tensor_tensor(out=sl(7), in0=sl(7), in1=sl(3), op=ALU.add), reads=["sm7", "sm3"], writes=["sm7"])
        else:
            sl = lambda i: S3(i)[:, :, 16:32]
            P.add("dve", lambda e: e.tensor_tensor(out=sl(7), in0=sl(5), in1=sl(3), op=ALU.add), reads=["sm456", "sm3"], writes=["sm7"])
        P.add("act", lambda e: e.activation(out=sl(7), in_=sl(7), func=AF.Exp), reads=["sm7"], writes=["sm7"])
        P.add("act", lambda e: e.activation(out=sl(8), in_=sl(6), func=AF.Exp), reads=["sm456"], writes=["sm8"])

    def tok_transposes(ci, xb=0):
        cols = slice(128 * ci, 128 * ci + 128)
        pT = banks[3][:].bitcast(BF16)
        xt = [xct(xb, t) for t in range(10)]

        def tr(e):
            for t in range(8):
                ins = e.transpose(out=pT[:, 128 * t:128 * t + 128], in_=xt[t][0][:, cols], identity=IDENT)
            return ins
        P.add("pe", tr, reads=[xt[t][1] for t in range(8)], writes=["bk3"])
        P.add("act", lambda e: e.activation(out=Xtok[:], in_=pT[:, :], func=AF.Copy), reads=["bk3"], writes=["Xtok"])
        pB = banks[4][:, 0:128].bitcast(BF16)

        def trb(e):
            for t in range(2):
                ins = e.transpose(out=pB[:, 128 * t:128 * t + 128], in_=xt[8 + t][0][:, cols], identity=IDENT)
            return ins
        P.add("pe", trb, reads=[xt[8][1], xt[9][1]], writes=["bk4.bt"])
        P.add("dve", lambda e: e.tensor_copy(out=Btok[:], in_=pB[:, :]), reads=["bk4.bt"], writes=["Btok"])

    def bc_h(ap16):
        return ap16[:, :, None].broadcast_to([128, 16, 64])

    def v3(ap):
        return ap.rearrange("p (h d) -> p h d", h=16)

    def smc(slot, ci, di):
        return sm[:, slot, 32 * ci + 16 * di:32 * ci + 16 * di + 16]

    def state_update(ci, di, S_banks):
        dec = smc(8, ci, di)
        P.add("dve", lambda e: e.tensor_tensor(out=v3(stt[:]), in0=v3(stt[:]), in1=bc_h(dec), op=ALU.mult),
              reads=["stt", "sm8"], writes=["stt"])
        for gi in range(2):
            P.add("dve", lambda e, gi=gi: e.tensor_tensor(out=stt[:, 512 * gi:512 * gi + 512], in0=stt[:, 512 * gi:512 * gi + 512],
                                                          in1=S_banks[gi][0][:, :], op=ALU.add),
                  reads=["stt"] + S_banks[gi][1], writes=["stt"])

    def state_matmuls(fixed=None):
        Sb = []
        for gi in range(2):
            bank, bres = next_pA() if fixed is None else (banks[fixed[gi]], bk(fixed[gi]))
            P.add("pe", lambda e, gi=gi, bank=bank: e.matmul(bank[:, :], lhsT=Btok[:, 128 * gi:128 * gi + 128],
                                                           rhs=XwS[:, 512 * gi:512 * gi + 512], start=True, stop=True),
                  reads=["Btok", "XwS"], writes=bres)
            Sb.append((bank, bres))
        return Sb

    hTs = [(hT, "hT"), (kT[:, :, 0:642], "kT")]

    def p1_norm_units(g, hbuf, hres):
        u2 = []
        for ci in range(4):
            def s1(ci=ci):
                c = 4 * g + ci
                xs, xr = load_x(slice(2 + 128 * c, 2 + 128 * c + 128), 128)
                norm_s1(xs, xr, 128, G_MIXPRE, ci % 2, use_pool=True)
            u2.append((s1, lambda ci=ci: norm_s2(128, hbuf, hres, 2 + 128 * ci)))

        def s1b():
            xs, xr = load_x(slice(512 * g, 512 * g + 2), 2)
            norm_s1(xs, xr, 2, G_MIXPRE, 2, use_pool=True)
        u2.append((s1b, lambda: norm_s2(2, hbuf, hres, 0)))
        return staged(u2)

    def p1_tile_units(g, hbuf, hres, xb):
        units = []
        st_ = {"pend": None, "wv": None}

        def tile(t):
            if t == 0 or t == 4:
                st_["wv"] = w_use((8, 512))
            elif t == 8:
                st_["wv"] = w_use((8, 256))
            wv, wr_ = st_["wv"]
            cur = tile_p1(t, wv, wr_, 128 * (t % 4), 2, 0, "L", hbuf, hres)
            if st_["pend"] is not None:
                tile_p2(*st_["pend"], xb=xb)
            st_["pend"] = cur
        for t in range(10):
            units.append(lambda t=t: tile(t))

        def fin():
            wdt, rdt = w_use((8, 288))
            bank, bres = next_pA()

            def mm(e):
                for ci in range(4):
                    for k in range(8):
                        ins = e.matmul(bank[:, 32 * ci:32 * ci + 32], lhsT=hbuf[:, k, 2 + 128 * ci:2 + 128 * ci + 128], rhs=wdt[:, k, 256:288],
                                       start=(ci == 0 and k == 0), stop=(k == 7), skip_group_check=True)
                return ins
            P.add("pe", mm, reads=[hres, rdt], writes=bres)
            tile_p2(*st_["pend"], xb=xb)
            P.add("dve", lambda e: e.tensor_copy(out=dtraw[:].rearrange("p c k -> p (c k)"), in_=bank[:, 0:128]), reads=bres, writes=["dtraw"])
            dt_prep_a()
        units.append(fin)
        return units

    def p1_chunk_units(g, xb):
        Sb_of = {}

        def cu1(ci):
            tok_transposes(ci, xb)
            P.add("dve", lambda e: e.tensor_tensor(out=v3(XwS[:]), in0=v3(Xtok[:]), in1=bc_h(smc(7, ci, 1)), op=ALU.mult),
                  reads=["Xtok", "sm7"], writes=["XwS"])

        def cu2(ci):
            c = 4 * g + ci
            Sb_of[ci] = state_matmuls(fixed=(5, 6))
            if c < 32:
                pi = c % 2
                P.add("act", lambda e: e.activation(out=prevb[pi][:], in_=stt[:], func=AF.Copy),
                      reads=["stt"], writes=[f"prevb{pi}"])


        def cu3(ci):
            c = 4 * g + ci
            if c < 32:
                pi = c % 2
                dma(QX, pb_s[c], prevb[pi][:], [f"prevb{pi}"], [f"pb{c}"], f"pbw{pi}")
            state_update(ci, 1, Sb_of[ci])
        order = (3, 2, 1, 0)
        units = []
        for i in range(6):
            def u(i=i):
                if 0 <= i - 2 < 4:
                    cu3(order[i - 2])
                if 0 <= i - 1 < 4:
                    cu2(order[i - 1])
                if i < 4:
                    cu1(order[i])
            units.append(u)
        return units

    def phase1_all():
        gs = list(range(n_groups1 - 1, -1, -1))
        if not gs:
            return
        for u in p1_norm_units(gs[0], *hTs[0]):
            u()
        prev_chunks = []
        pending_b = False
        for idx, g in enumerate(gs):
            A = p1_tile_units(g, *hTs[idx % 2], idx % 2)
            B = p1_norm_units(gs[idx + 1], *hTs[(idx + 1) % 2]) if idx + 1 < len(gs) else []
            C = prev_chunks
            bslot = {0: 0, 2: 1, 4: 2, 6: 3, 8: 4, 10: 5}
            cslot = {0: 0, 1: 1, 3: 2, 5: 3, 7: 4, 9: 5}
            for i in range(len(A)):
                A[i]()
                if i == 0 and pending_b:
                    dt_prep_b(False)
                    pending_b = False
                if i in (1, 3, 5, 7, 9):
                    emit_casts(1)
                if i in cslot and cslot[i] < len(C):
                    C[cslot[i]]()
                if i in bslot and bslot[i] < len(B):
                    B[bslot[i]]()
            assert len(C) <= 6 and len(B) <= 6 and len(A) == 11
            prev_chunks = p1_chunk_units(g, idx % 2)
            pending_b = True
        if pending_b:
            dt_prep_b(False)
        for u in prev_chunks:
            u()
        fence = [f"xq{t}" for t in range(10)] + ["qT", "ycatT0", "ycatT1", "kT", "hT"] + [f"xc{t}" for t in range(12)]
        P.add("pool", lambda e: e.memset(stt[:], 0.0), reads=["stt"], writes=["stt"] + fence)

    def p2_norm_units(g):
        held = {}

        def load(ci):
            c = 4 * g + ci
            held[ci] = load_x(slice(2 + 128 * c, 2 + 128 * c + 128), 128)

        def s1b(ci):
            xs, xr = held[ci]
            norm_s1(xs, xr, 128, G_MIXPRE, ci % 2)

        def s2(ci):
            norm_s2(128, hT, "hT", 2 + 128 * ci)
        units = []
        for k in range(7):
            def u(k=k):
                if 0 <= k - 2 < 5:
                    s2(k - 2)
                if 0 <= k - 1 < 5:
                    s1b(k - 1)
                if k < 5:
                    load(k)
            units.append(u)
        return units

    def stageA(g, inline_norm):
        if inline_norm:
            for u in p2_norm_units(g):
                u()
        pend = None
        for t in range(28):
            if t % 4 == 0:
                wv, wres = w_use((8, 512))
            wc = 128 * (t % 4)
            if t < 12:
                cur = tile_p1(t, wv, wres, wc, 2, 514, "R")
                if pend is not None:
                    tile_p2(*pend)
                pend = cur
                continue
            bank, bres = next_pA()

            def mm(e, bank=bank, wv=wv, wc=wc):
                for k in range(8):
                    ins = e.matmul(bank[:, :], lhsT=wv[:, k, wc:wc + 128], rhs=hT[:, k, 2:514], start=(k == 0), stop=(k == 7))
                return ins
            P.add("pe", mm, reads=[wres, "hT"], writes=bres)
            if pend is not None:
                tile_p2(*pend)
                pend = None
            if t < 20:
                P.add("act", lambda e, bank=bank, t=t: e.activation(out=qT[:, t - 12, :], in_=bank[:, :], func=AF.Copy, scale=0.125),
                      reads=bres, writes=["qT"])
            else:
                P.add("dve", lambda e, bank=bank, t=t: e.tensor_copy(out=kT[:, t - 20, 128:640], in_=bank[:, :]),
                      reads=bres, writes=["kT"])
                bank2, bres2 = next_pA()

                def mm2(e, bank2=bank2, wv=wv, wc=wc):
                    for k in range(8):
                        ins = e.matmul(bank2[:, 0:128], lhsT=wv[:, k, wc:wc + 128], rhs=hT[:, k, 514:642], start=(k == 0), stop=(k == 7))
                    return ins
                P.add("pe", mm2, reads=[wres, "hT"], writes=bres2)
                P.add("dve", lambda e, bank2=bank2, t=t: e.tensor_copy(out=kT[:, t - 20, 640:768], in_=bank2[:, 0:128]),
                      reads=bres2, writes=["kT"])
        for zp in range(2):
            wv, wres = w_use((8, 512))
            for ci in range(4):
                bank, bres = next_pA()

                def mm(e, bank=bank, wv=wv, ci=ci):
                    for k in range(8):
                        ins = e.matmul(bank[:, :], lhsT=hT[:, k, 2 + 128 * ci:2 + 128 * ci + 128], rhs=wv[:, k, :], start=(k == 0), stop=(k == 7))
                    return ins
                P.add("pe", mm, reads=[wres, "hT"], writes=bres)
                P.add("act", lambda e, bank=bank, ci=ci, zp=zp: e.activation(out=zs[:, ci, 512 * zp:512 * zp + 512], in_=bank[:, :], func=AF.Silu),
                      reads=bres, writes=["zs"])
        wv, wres = w_use((8, 288))
        for ci in range(5):
            bank, bres = next_pA()

            def mm(e, bank=bank, wv=wv, ci=ci):
                for k in range(8):
                    ins = e.matmul(bank[:, 0:288], lhsT=hT[:, k, 2 + 128 * ci:2 + 128 * ci + 128], rhs=wv[:, k, :], start=(k == 0), stop=(k == 7))
                return ins
            P.add("pe", mm, reads=[wres, "hT"], writes=bres)
            P.add("act", lambda e, bank=bank, ci=ci: e.activation(out=vv[:, ci + 1, :], in_=bank[:, 0:256], func=AF.Copy),
                  reads=bres, writes=["vv"])
            if ci < 4:
                P.add("dve", lambda e, bank=bank, ci=ci: e.tensor_copy(out=dtraw[:, ci, :], in_=bank[:, 256:288]),
                      reads=bres, writes=["dtraw"])
        dt_prep_group(fwd=True)

    def attn_units(g, ci, slot):
        c = 4 * g + ci
        qcols = slice(128 * ci, 128 * ci + 128)
        blocks = [j for j in range(3) if not (c == 0 and j == 0)]
        units = []
        for kg in range(4):
            for j in blocks:
                def score(kg=kg, j=j):
                    kcols = slice(128 * (ci + j), 128 * (ci + j) + 128)

                    def mm(e):
                        e.matmul(banks[2][:, :], lhsT=IDENT, rhs=abias[:, kg, j, :], start=True, stop=False, skip_group_check=True)
                        e.matmul(banks[2][:, 0:256], lhsT=kT[:, kg, kcols], rhs=qT[:, 2 * kg:2 * kg + 2, qcols], start=False, stop=False, skip_group_check=True)
                        return e.matmul(banks[2][:, 256:512], lhsT=kT[:, 4 + kg, kcols], rhs=qT[:, 2 * kg:2 * kg + 2, qcols], start=False, stop=True, skip_group_check=True)
                    P.add("pe", mm, reads=["abias", "cbf", "kT", "qT"], writes=["bk2"])
                    P.add("act", lambda e: e.activation(out=PT[:, j, :], in_=banks[2][:, :], func=AF.Exp),
                          reads=["bk2"], writes=[f"PT{j}"])
                units.append(score)

            def pvn(kg=kg):
                po = banks[5]

                def pv(e):
                    first = True
                    for j in blocks:
                        vsl = vv[:, ci + j, 64 * kg:64 * kg + 64]
                        for m in range(2):
                            rh = PT[:, j, 256 * m:256 * m + 256]
                            e.matmul(po[64 * m:64 * m + 64, 0:256], lhsT=vsl, rhs=rh, start=first, stop=False, skip_group_check=True)
                            ins = e.matmul(po[64 * m:64 * m + 64, 256:512], lhsT=ONES[:, 0:64], rhs=rh, start=False, stop=False, skip_group_check=True)
                        first = False
                    return ins
                P.add("pe", pv, reads=["vv", "cbf"] + [f"PT{j}" for j in blocks], writes=["bk5"])
                for pr in range(2):
                    P.add("act", lambda e, pr=pr: e.activation(out=rden[:, 128 * pr:128 * pr + 128], in_=po[:, 256 + 128 * pr:256 + 128 * pr + 128],
                                                               func=AF.Ln, bias=esk[:, 2 * kg + pr:2 * kg + pr + 1]),
                          reads=["bk5", "esk"], writes=["rden"])
                P.add("act", lambda e: e.activation(out=rden[:], in_=rden[:], func=AF.Exp, scale=-1.0), reads=["rden"], writes=["rden"])
                P.add("dve", lambda e: e.tensor_tensor(out=ycatT[:, slot, 8 + 2 * kg:8 + 2 * kg + 2, :],
                                                       in0=po[:, 0:256].rearrange("p (a q) -> p a q", a=2),
                                                       in1=rden[:].rearrange("p (a q) -> p a q", a=2), op=ALU.mult),
                      reads=["bk5", "rden"], writes=[f"ycatT{slot}"])
            units.append(pvn)
        return units

    def ssd_attn_chunk(g, ci, slot, with_attn=True):
        c = 4 * g + ci
        cols = slice(128 * ci, 128 * ci + 128)
        pbi = c % 2
        units = attn_units(g, ci, slot) if with_attn else []
        dma(QX, prevb[pbi][:], pb_s[c], [f"pb{c}"], [f"prevb{pbi}"], f"pbr{pbi}")
        pcb = banks[4][:, 256:512]

        def mmcb(e):
            for gi in range(2):
                ins = e.matmul(pcb[:, 128 * gi:128 * gi + 128], lhsT=xc[:, 8 + gi, cols], rhs=xc[:, 10 + gi, cols], start=True, stop=True)
            return ins
        P.add("pe", mmcb, reads=["xc8", "xc9", "xc10", "xc11"], writes=["bk4.cb"])
        pcb3 = pcb.rearrange("p (g l) -> p g l", g=2)
        P.add("dve", lambda e: e.tensor_tensor(out=CBm[:, 0, :, :], in0=pcb3, in1=UU[:, None, :].broadcast_to([128, 2, 128]), op=ALU.mult),
              reads=["bk4.cb", "cbf"], writes=["CBm0"])
        P.add("dve", lambda e: e.tensor_tensor(out=CBm[:, 1, :, :], in0=pcb3, in1=UB[:, None, :].broadcast_to([128, 2, 128]), op=ALU.mult),
              reads=["bk4.cb", "cbf"], writes=["CBm1"])
        tok_transposes(ci)
        dt_f, dt_b = smc(1, ci, 0), smc(1, ci, 1)
        for di in range(2):
            umat = UU if di == 0 else UB
            dAb = smb[:, 32 * ci + 16 * di:32 * ci + 16 * di + 16]
            ra, rres = rhs_alls[di]
            P.add("dve" if di == 0 else "pool",
                  lambda e, umat=umat, dAb=dAb, ra=ra: e.tensor_tensor(out=ra, in0=umat[:, None, :].broadcast_to([128, 16, 128]),
                                                                      in1=dAb[:, :, None].broadcast_to([128, 16, 128]), op=ALU.mult),
                  reads=["cbf", "smb"], writes=[rres])
            if di == 0:
                P.add("pool", lambda e: e.tensor_tensor(out=v3(XwS[:]), in0=v3(Xtok[:]), in1=bc_h(smc(7, ci, 0)), op=ALU.mult),
                      reads=["Xtok", "sm7"], writes=["XwS"])
                P.add("dve", lambda e: e.tensor_tensor(out=v3(XDs[:]), in0=v3(Xtok[:]), in1=v3(dskip[:]), op=ALU.mult),
                      reads=["Xtok", "dskip"], writes=["XDs"])
                P.add("dve", lambda e: e.tensor_tensor(out=v3(Xdf[:]), in0=v3(Xtok[:]), in1=bc_h(dt_f), op=ALU.mult),
                      reads=["Xtok", "sm1"], writes=["Xdf"])
        P.add("pool", lambda e: e.tensor_tensor(out=v3(Xdb[:]), in0=v3(Xtok[:]), in1=bc_h(dt_b), op=ALU.mult),
              reads=["Xtok", "sm1"], writes=["Xdb"])
        n_pre = min(5, len(units)) if INTERLEAVE else 0
        ui = 0
        while ui < n_pre:
            units[ui]()
            ui += 1
        Y = [banks[6], banks[7]]
        YR = bk(6) + bk(7)

        def y0(e):
            e.matmul(Y[0][:, :], lhsT=IDENT, rhs=XDs[:, 0:512], start=True, stop=False, skip_group_check=True)
            return e.matmul(Y[1][:, :], lhsT=IDENT, rhs=XDs[:, 512:1024], start=True, stop=False, skip_group_check=True)
        P.add("pe", y0, reads=["XDs", "cbf"], writes=YR)
        pfi = c % 2
        steps = []
        for di in range(2):
            for kind in ("D", "R"):
                if kind == "R" and di == 0 and c == 0:
                    continue
                for q in range(4):
                    steps.append((di, kind, q))
        last_dir = [None]

        def emit_D(step):
            di, kind, q = step
            rhs_all, rares = rhs_alls[di]
            if kind == "D":
                lhs = SG if di == 0 else SU
            else:
                lhs = ONES
            bi = (0, 1, 3, 4)[cnt["pd"] % 4]
            cnt["pd"] += 1
            bank, bres = banks[bi], (bk(bi) if bi != 4 else BK4)
            P.add("pe", lambda e: e.matmul(bank[:, :], lhsT=lhs, rhs=rhs_all[:, 4 * q:4 * q + 4, :], start=True, stop=True),
                  reads=[rares, "cbf"], writes=bres)
            ei = cnt["er"] % 5
            cnt["er"] += 1
            E, eres = Ering[ei], f"Er{ei}"
            P.add("act", lambda e: e.activation(out=E[:], in_=bank[:, :].rearrange("p (h l) -> p h l", h=4), func=AF.Exp),
                  reads=bres, writes=[eres])
            gi = q // 2
            if kind == "D":
                m_in = CBm[:, di, gi, :]
                mres = [f"CBm{di}"]
            else:
                m_in = xc[:, 10 + gi, cols]
                mres = [f"xc{10 + gi}"]
            P.add("dve", lambda e: e.tensor_tensor(out=E[:], in0=E[:], in1=m_in[:, None, :].broadcast_to([128, 4, 128]), op=ALU.mult),
                  reads=[eres] + mres, writes=[eres])
            return E, eres

        def emit_Y(step, E, eres):
            di, kind, q = step
            if kind == "D":
                R, rres = (Xdf, "Xdf") if di == 0 else (Xdb, "Xdb")
            else:
                R, rres = (prevf[pfi], f"prevf{pfi}") if di == 0 else (prevb[pbi], f"prevb{pbi}")

            def mm(e):
                for hh in range(4):
                    h = 4 * q + hh
                    ins = e.matmul(Y[h // 8][:, 64 * (h % 8):64 * (h % 8) + 64], lhsT=E[:, hh, :], rhs=R[:, 64 * h:64 * h + 64],
                                   start=False, stop=False, skip_group_check=True)
                return ins
            P.add("pe", mm, reads=[eres, rres], writes=YR)

        n_tail = 5 if INTERLEAVE else 0
        LA = 4
        pend = [emit_D(steps[k]) for k in range(min(LA, len(steps)))]
        for i in range(len(steps)):
            if INTERLEAVE and ui < len(units) - n_tail:
                units[ui]()
                ui += 1
            if i + LA < len(steps):
                pend.append(emit_D(steps[i + LA]))
            emit_Y(steps[i], *pend.pop(0))
        P.add("dve", lambda e: e.tensor_tensor(out=gbuf[:, 0:512], in0=banks[6][:, :], in1=zs[:, ci, 0:512], op=ALU.mult),
              reads=bk(6) + ["zs"], writes=["gbuf"])
        P.add("dve", lambda e: e.tensor_tensor(out=gbuf[:, 512:1024], in0=banks[7][:, :], in1=zs[:, ci, 512:1024], op=ALU.mult),
              reads=bk(7) + ["zs"], writes=["gbuf"])
        norm_s1(gbuf[:], "gbuf", 128, G_SSM, 3)
        Sf = state_matmuls()
        while ui < len(units):
            units[ui]()
            ui += 1
        state_update(ci, 0, Sf)
        nfi = (c + 1) % 2
        P.add("act", lambda e: e.activation(out=prevf[nfi][:], in_=stt[:], func=AF.Copy), reads=["stt"], writes=[f"prevf{nfi}"])
        norm_s2(128, ycatT[:, slot, 0:8, :], f"ycatT{slot}", 0)

    def rstd_from(sidx_list, dst_idx, nel):
        d = stat[:, dst_idx:dst_idx + 1]
        a, b = sidx_list
        P.add("dve", lambda e: e.tensor_tensor(out=d, in0=stat[:, a:a + 1], in1=stat[:, b:b + 1], op=ALU.add),
              reads=[f"stat{a}", f"stat{b}"], writes=[f"stat{dst_idx}"])
        P.add("act", lambda e: e.activation(out=d, in_=d, func=AF.Ln, scale=1.0 / nel, bias=EPSB[:]),
              reads=[f"stat{dst_idx}"], writes=[f"stat{dst_idx}"])
        P.add("act", lambda e: e.activation(out=d, in_=d, func=AF.Exp, scale=-0.5), reads=[f"stat{dst_idx}"], writes=[f"stat{dst_idx}"])

    def resid_norm_add(acc_bank, gain, s):
        for hf in range(2):
            P.add("act", lambda e, hf=hf: e.activation(out=hb[0][:, 512 * hf:512 * hf + 512], in_=acc_bank[hf][:, :], func=AF.Square,
                                                      accum_out=stat[:, 4 + hf:5 + hf]),
                  reads=ACCB, writes=["hb0", f"stat{4 + hf}"])
            P.add("dve", lambda e, hf=hf: e.tensor_tensor(out=gbuf[:, 512 * hf:512 * hf + 512], in0=acc_bank[hf][:, :],
                                                         in1=gain[:, 512 * hf:512 * hf + 512], op=ALU.mult),
                  reads=ACCB + ["gains"], writes=["gbuf"])
        rstd_from([4, 5], 6, D)
        P.add("dve", lambda e: e.scalar_tensor_tensor(out=x1[:, s, :], in0=gbuf[:], scalar=stat[:, 6:7], in1=x1[:, s, :],
                                                      op0=ALU.mult, op1=ALU.add),
              reads=["gbuf", "stat6", "x1"], writes=["x1"])

    def outproj_pair(g, pair):
        c0 = 4 * g + 2 * pair
        dma(QX, x1[:], x_d[2 + 128 * c0:2 + 128 * c0 + 256, :].rearrange("(s p) d -> p s d", p=128), [], ["x1"], "x1")
        for i in (2, 3, 0, 1):
            wv, wres = w_use((4, 1024))

            def mm(e, wv=wv, i=i):
                for kk in range(4):
                    kt = 4 * i + kk
                    for s in range(2):
                        for hf in range(2):
                            ins = e.matmul(banks[4 + 2 * s + hf][:, :], lhsT=ycatT[:, s, kt, :], rhs=wv[:, kk, 512 * hf:512 * hf + 512],
                                           start=(kt == 8), stop=(kt == 7))
                return ins
            P.add("pe", mm, reads=[wres, "ycatT0", "ycatT1"], writes=ACCB)
        for s in range(2):
            resid_norm_add([banks[4 + 2 * s], banks[5 + 2 * s]], G_MIXPOST, s)

    def ffn_pair(g, pair, hoist_next):
        c0 = 4 * g + 2 * pair
        hoist = p2_norm_units(g + 1) if hoist_next else []
        for s in range(2):
            norm_T(x1[:, s, :], "x1", 128, G_MLPPRE, h2T, "h2T", 128 * s, 7)
        acc = [[banks[4 + 2 * s + hf] for hf in range(2)] for s in range(2)]
        upq = []
        state = {"wu": None, "wd": None}

        def emit_up(j):
            u, jj = divmod(j, 4)
            if jj == 0:
                state["wu"] = w_use((8, 512))
                state["wd"] = w_use((4, 1024))
            wu, ru = state["wu"]
            wd, rd = state["wd"]
            pi = cnt["pu"] % 3
            cnt["pu"] += 1
            ub = banks[pi][:, 0:256]
            ures = [f"bk{pi}"]

            def mmu(e):
                for k in range(8):
                    ins = e.matmul(ub, lhsT=wu[:, k, 128 * jj:128 * jj + 128], rhs=h2T[:, k, :], start=(k == 0), stop=(k == 7))
                return ins
            P.add("pe", mmu, reads=[ru, "h2T"], writes=ures)
            fi = cnt["fr"] % 4
            cnt["fr"] += 1
            fr, fres = fring[fi], f"fr{fi}"
            P.add("act", lambda e: e.activation(out=fr[:], in_=ub, func=AF.Relu), reads=ures, writes=[fres])
            P.add("dve", lambda e: e.tensor_tensor(out=fr[:], in0=fr[:], in1=fr[:], op=ALU.mult), reads=[fres], writes=[fres])
            upq.append((fr, fres, wd, rd, jj, j))

        def emit_down():
            fr, fres, wd, rd, jj, j = upq.pop(0)

            def mmd(e):
                for s in range(2):
                    for hf in range(2):
                        ins = e.matmul(acc[s][hf][:, :], lhsT=fr[:, 128 * s:128 * s + 128], rhs=wd[:, jj, 512 * hf:512 * hf + 512],
                                       start=(j == 0), stop=(j == 31))
                return ins
            P.add("pe", mmd, reads=[rd, fres], writes=ACCB)

        for j in range(32):
            emit_up(j)
            if j % 4 >= 2:
                emit_down()
            if j % 4 == 3:
                emit_down()
                emit_down()
                if hoist:
                    hoist.pop(0)()
        assert not upq
        while hoist:
            hoist.pop(0)()
        for s in range(2):
            c = c0 + s
            resid_norm_add(acc[s], G_MLPPOST, s)
            dma(QX, out_d[128 * c:128 * c + 128, :], x1[:, s, :], ["x1"], [f"out{c}"], "outd")

    def carry_kv():
        P.add("pool", lambda e: e.tensor_copy(out=kT[:, :, 0:128], in_=kT[:, :, 512:640]), reads=["kT"], writes=["kT"])
        P.add("pool", lambda e: e.tensor_copy(out=vv[:, 0, :], in_=vv[:, 4, :]), reads=["vv"], writes=["vv"])

    def chunks(g, pair, wa=True):
        for s_ in range(2):
            ssd_attn_chunk(g, 2 * pair + s_, s_, with_attn=wa)

    def phase2_group(g):
        last = (g + 1 >= n_groups2)
        if not PIPE2:
            stageA(g, inline_norm=(g == 0 or stop is not None))
            if stop == "A":
                return
            for pair in range(2):
                chunks(g, pair, wa=(stop != "ssd"))
                if stop in ("ssd", "attn"):
                    continue
                outproj_pair(g, pair)
                if stop == "outproj":
                    continue
                ffn_pair(g, pair, hoist_next=(pair == 1 and not last and stop is None))
            carry_kv()
            return
        if g == 0:
            stageA(0, inline_norm=True)
        chunks(g, 0)
        outproj_pair(g, 0)
        chunks(g, 1)
        carry_kv()
        ffn_pair(g, 0, hoist_next=not last)
        outproj_pair(g, 1)
        if not last:
            stageA(g + 1, inline_norm=False)
        ffn_pair(g, 1, hoist_next=False)

    setup()
    phase1_all()
    emit_casts(1000)
    if n_groups2 > 0:
        P.add("pool", lambda e: e.memset(stt[:], 0.0), reads=["stt"], writes=["stt"])
        P.add("pool", lambda e: e.memset(ucarry[:], 0.0), writes=[f"uc{t}" for t in range(12)])
    for g in range(n_groups2):
        phase2_group(g)
    outs = ([f"out{c}" for c in range(4 * n_groups2)] if stop is None else []) + ["dbg_" + k for k in dbg_outs]
    P.add("sp", None, reads=outs + [f"pb{c}" for c in range(min(32, 4 * n_groups1))])
    P.finalize(st)
    P.emit()
    print("IR ops:", len(P.ops))
    return nc, st, dbg_outs


def _consts():
    a = np.arange(128)[:, None]
    b = np.arange(128)[None, :]
    mats = [a == b, a <= b, a >= b, a < b, a > b, np.ones((128, 128), bool)]
    cbf = np.stack([m.astype(np.float32) for m in mats], 1).astype(NPBF)
    cf32 = np.stack([(a <= b), (a < b), np.ones((128, 128), bool)], 1).astype(np.float32)
    slopes = 2.0 ** (-8.0 * np.arange(1, 17, dtype=np.float64) / 16)
    ab = np.zeros((128, 4, 3, 4, 128), np.float32)
    k = np.arange(128)[:, None]
    q = np.arange(128)[None, :]
    for g in range(4):
        for j in range(3):
            dist = (k + 128 * (j - 1)) - q
            valid = np.abs(dist) <= 128
            for pos, r in enumerate((0, 2, 1, 3)):
                h = 4 * g + r
                ab[:, g, j, pos, :] = np.where(valid, -slopes[h] * np.abs(dist), -30000.0)
    return cbf, cf32, ab.reshape(128, 4, 3, 512).astype(NPBF)


def _prep_core(inp, b, half):
    f = lambda a: np.ascontiguousarray(np.asarray(a, dtype=np.float32))
    x = f(inp["x"])[b]
    w_in = f(inp["w_in"])[0]
    conv_w = f(inp["conv_w"])[0]
    dt_bias = f(inp["dt_bias"])[0]
    a_log = f(inp["a_log"])[0]
    if half == 1:
        x = x[::-1]
        conv_w = conv_w[::-1]
        dt_bias = dt_bias[::-1]
        a_log = a_log[::-1]
    xp = np.zeros((NLOC + 2, D), np.float32)
    xp[2:] = x
    z = w_in[:, 0:1024]
    xs = w_in[:, 1024:2048]
    Bc = w_in[:, 2048:2304]
    Cc = w_in[:, 2304:2560]
    dtf = w_in[:, 2560:2576]
    dtb_ = w_in[:, 2576:2592]
    q = w_in[:, 2592:3616]
    k = w_in[:, 3616:3872]
    v = w_in[:, 3872:4128]
    zz = np.zeros((D, 64), np.float32)
    kd = np.concatenate([np.concatenate([k[:, 64 * g:64 * g + 64], zz], 1) for g in range(4)]
                        + [np.concatenate([zz, k[:, 64 * g:64 * g + 64]], 1) for g in range(4)], 1)
    dts = [dtf, dtb_] if half == 0 else [dtb_, dtf]
    win = np.concatenate([xs, Bc, Cc, q, kd, z, v] + dts, 1)
    assert win.shape == (D, WIN_COLS)
    rep = lambda vec: np.ascontiguousarray(np.broadcast_to(vec[None], (128,) + vec.shape))
    gains = np.stack([f(inp[n])[0] for n in ("norm_mix_pre", "ssm_norm", "norm_mix_post", "norm_mlp_pre", "norm_mlp_post")], 0)
    sink = f(inp["attn_sink"])[0]
    sink_t = np.zeros((128, 8), np.float32)
    for pr in range(8):
        sink_t[0:64, pr] = sink[2 * pr]
        sink_t[64:128, pr] = sink[2 * pr + 1]
    cw = conv_w.T.reshape(12, 128, 5).transpose(1, 0, 2)
    cb = f(inp["conv_b"])[0].reshape(12, 128).T
    return {
        "x": xp, "w_in": np.ascontiguousarray(win), "w_out": f(inp["w_out"])[0], "w_up": f(inp["w_up"])[0],
        "w_down": f(inp["w_down"])[0], "gains": rep(gains), "dskip": rep(np.repeat(f(inp["d_skip"])[0], 64)),
        "convw": np.ascontiguousarray(cw), "convb": np.ascontiguousarray(cb),
        "dtb": rep(dt_bias.reshape(32)), "alog": rep(a_log.reshape(32)), "sink": sink_t,
    }


_CACHE = {}


def kernel(**inputs):
    if "nc" not in _CACHE:
        _CACHE["nc"] = build()
    nc, st, _ = _CACHE["nc"]
    cbf, cf32, abias = _consts()
    in_maps = []
    for core in range(8):
        b, half = core // 2, core % 2
        m = _prep_core(inputs, b, half)
        m.update({"cbf": cbf, "cf32": cf32, "abias": abias})
        in_maps.append(m)
    res = run_bass_kernel_spmd(nc, in_maps, core_ids=list(range(8)))
    out = np.zeros((4, 8192, D), np.float32)
    for core in range(8):
        b, half = core // 2, core % 2
        o = np.asarray(res.results[core]["out"], dtype=np.float32)
        if half == 0:
            out[b, 0:NOWN] = o
        else:
            out[b, NOWN:] = o[::-1]
    return out
```
